# Optimizing a Trainium2 kernel written in Bass

```python
import math
import jax
import jax.numpy as jnp
from jax import lax
import numpy as np

D_MODEL = 1024
BATCH = 8
SEQ = 4096
DEPTH = 4

D_MIX = 2 * D_MODEL
D_SSD = D_MODEL
D_GMLP = D_MIX - D_SSD
SSD_HEAD_DIM = 64
SSD_HEADS = D_SSD // SSD_HEAD_DIM
SSD_GROUPS = 4
HEADS_PER_GROUP = SSD_HEADS // SSD_GROUPS
D_STATE = 128
CONV_WIDTH = 5
CONV_DIM = D_SSD + 2 * SSD_GROUPS * D_STATE
CHUNK = 128
GMLP_GROUPS = 8
GMLP_GROUP_DIM = D_GMLP // GMLP_GROUPS
D_FF = ((8 * D_MODEL // 3 + 127) // 128) * 128
FFN_RESIDUAL_WEIGHT = 0.5
D_IN_PROJ = D_SSD + CONV_DIM + 2 * SSD_HEADS + 2 * D_GMLP
SPLIT_IDX = (D_SSD,
             D_SSD + CONV_DIM,
             D_SSD + CONV_DIM + SSD_HEADS,
             D_SSD + CONV_DIM + 2 * SSD_HEADS,
             D_SSD + CONV_DIM + 2 * SSD_HEADS + D_GMLP)
EPS = 1e-6

kernel_name = 'hybrid_ssd_gmlp_macaron_sandwich_encoder'


def rms_norm(x, g):
    xf = x.astype(jnp.float32)
    xf = xf * lax.rsqrt(jnp.mean(xf * xf, axis=-1, keepdims=True) + EPS)
    return (xf * g.astype(jnp.float32)).astype(x.dtype)


def layer_norm(x, g, b):
    xf = x.astype(jnp.float32)
    mu = jnp.mean(xf, axis=-1, keepdims=True)
    xc = xf - mu
    xf = xc * lax.rsqrt(jnp.mean(xc * xc, axis=-1, keepdims=True) + EPS)
    return (xf * g.astype(jnp.float32) + b.astype(jnp.float32)).astype(x.dtype)


def swiglu(h, w_gate, w_up, w_down):
    return (jax.nn.silu(h @ w_gate) * (h @ w_up)) @ w_down


def centred_depthwise_conv(x, w, b):
    pad = (CONV_WIDTH - 1) // 2
    y = lax.conv_general_dilated(
        x, w[:, None, :].astype(x.dtype), window_strides=(1,),
        padding=[(pad, pad)], dimension_numbers=('NWC', 'WIO', 'NWC'),
        feature_group_count=x.shape[-1])
    return y + b


def ssd_chunked(x, dt, a, bm, cm):
    bsz, seqlen = x.shape[:2]
    nc = seqlen // CHUNK
    xc = x.reshape(bsz, nc, CHUNK, SSD_GROUPS, HEADS_PER_GROUP, SSD_HEAD_DIM)
    dtc = dt.reshape(bsz, nc, CHUNK, SSD_GROUPS, HEADS_PER_GROUP)
    bc = bm.reshape(bsz, nc, CHUNK, SSD_GROUPS, D_STATE)
    cc = cm.reshape(bsz, nc, CHUNK, SSD_GROUPS, D_STATE)
    cs = jnp.cumsum(dtc * a, axis=2)
    xdt = xc * dtc[..., None]
    seg = cs[:, :, :, None] - cs[:, :, None, :]
    lower = jnp.tril(jnp.ones((CHUNK, CHUNK), dtype=bool))[:, :, None, None]
    lmat = jnp.exp(jnp.where(lower, seg, -jnp.inf))
    cb = jnp.einsum('bcign,bcjgn->bcijg', cc, bc)
    y_diag = jnp.einsum('bcijgr,bcjgrp->bcigrp', cb[..., None] * lmat, xdt)
    decay_states = jnp.exp(cs[:, :, -1:] - cs)
    states = jnp.einsum('bcjgn,bcjgr,bcjgrp->bcgrpn', bc, decay_states, xdt)
    chunk_decay = jnp.exp(cs[:, :, -1])

    def step(carry, inp):
        s, d = inp
        return carry * d[..., None, None] + s, carry

    init = jnp.zeros_like(states[:, 0])
    _, prev = lax.scan(step, init, (jnp.moveaxis(states, 1, 0), jnp.moveaxis(chunk_decay, 1, 0)))
    prev = jnp.moveaxis(prev, 0, 1)
    y_off = jnp.einsum('bcign,bcgrpn->bcigrp', cc, prev) * jnp.exp(cs)[..., None]
    return (y_diag + y_off).reshape(bsz, seqlen, SSD_GROUPS, HEADS_PER_GROUP, SSD_HEAD_DIM)


def token_mix(h, w_in, conv_w, conv_b, dt_bias_f, dt_bias_b, a_log_f, a_log_b, d_skip,
              ssd_norm_g, gmlp_ln_g, gmlp_ln_b, spatial_w, spatial_b, w_out):
    bsz, seqlen, _ = h.shape
    f32 = jnp.float32
    proj = h @ w_in
    z, xbc, dt_f, dt_b, u, v = jnp.split(proj, SPLIT_IDX, axis=-1)

    xbc = jax.nn.silu(centred_depthwise_conv(xbc, conv_w, conv_b))
    xs, bm, cm = jnp.split(xbc, (D_SSD, D_SSD + SSD_GROUPS * D_STATE), axis=-1)
    xs = xs.astype(f32).reshape(bsz, seqlen, SSD_GROUPS, HEADS_PER_GROUP, SSD_HEAD_DIM)
    bm = bm.astype(f32).reshape(bsz, seqlen, SSD_GROUPS, D_STATE)
    cm = cm.astype(f32).reshape(bsz, seqlen, SSD_GROUPS, D_STATE)
    hshape = (bsz, seqlen, SSD_GROUPS, HEADS_PER_GROUP)
    dtf = jax.nn.softplus(dt_f.astype(f32) + dt_bias_f.astype(f32)).reshape(hshape)
    dtb = jax.nn.softplus(dt_b.astype(f32) + dt_bias_b.astype(f32)).reshape(hshape)
    a_f = -jnp.exp(a_log_f.astype(f32)).reshape(SSD_GROUPS, HEADS_PER_GROUP)
    a_b = -jnp.exp(a_log_b.astype(f32)).reshape(SSD_GROUPS, HEADS_PER_GROUP)
    flip = lambda t: jnp.flip(t, axis=1)
    y_fwd = ssd_chunked(xs, dtf, a_f, bm, cm)
    y_bwd = flip(ssd_chunked(flip(xs), flip(dtb), a_b, flip(bm), flip(cm)))
    d = d_skip.astype(f32).reshape(SSD_GROUPS, HEADS_PER_GROUP)[..., None]
    y = (y_fwd + y_bwd + d * xs).reshape(bsz, seqlen, D_SSD)
    y_ssd = rms_norm(y * jax.nn.silu(z.astype(f32)), ssd_norm_g).astype(h.dtype)

    u = jax.nn.gelu(u, approximate=False)
    v = layer_norm(jax.nn.gelu(v, approximate=False), gmlp_ln_g, gmlp_ln_b)
    nc = seqlen // CHUNK
    vc = v.reshape(bsz, nc, CHUNK, GMLP_GROUPS, GMLP_GROUP_DIM)
    mixed = jnp.einsum('gij,bcjgd->bcigd', spatial_w, vc) + spatial_b.T[None, None, :, :, None]
    y_gmlp = u * mixed.reshape(bsz, seqlen, D_GMLP)

    return jnp.concatenate([y_ssd, y_gmlp], axis=-1) @ w_out


def setup_inputs(seed: int = 0) -> dict:
    key = jax.random.key(seed)
    ks = list(jax.random.split(key, 32))
    nrm = lambda k, shape, scale: scale * jax.random.normal(k, shape, jnp.float32)
    gain = lambda k, shape: 1.0 + 0.02 * jax.random.normal(k, shape, jnp.float32)
    L = DEPTH
    u_dt_f = jax.random.uniform(ks[10], (L, SSD_HEADS), jnp.float32)
    u_dt_b = jax.random.uniform(ks[11], (L, SSD_HEADS), jnp.float32)
    lo, hi = math.log(1e-3), math.log(1e-1)
    dt_f0 = jnp.exp(u_dt_f * (hi - lo) + lo)
    dt_b0 = jnp.exp(u_dt_b * (hi - lo) + lo)
    return {
        'x': jax.random.normal(ks[0], (BATCH, SEQ, D_MODEL), jnp.float32),
        'ff1_pre_g': gain(ks[1], (L, D_MODEL)),
        'ff1_w_gate': nrm(ks[2], (L, D_MODEL, D_FF), D_MODEL ** -0.5),
        'ff1_w_up': nrm(ks[3], (L, D_MODEL, D_FF), D_MODEL ** -0.5),
        'ff1_w_down': nrm(ks[4], (L, D_FF, D_MODEL), D_FF ** -0.5),
        'ff1_post_g': gain(ks[5], (L, D_MODEL)),
        'mix_pre_g': gain(ks[6], (L, D_MODEL)),
        'w_in': nrm(ks[7], (L, D_MODEL, D_IN_PROJ), D_MODEL ** -0.5),
        'conv_w': nrm(ks[8], (L, CONV_WIDTH, CONV_DIM), CONV_WIDTH ** -0.5),
        'conv_b': nrm(ks[9], (L, CONV_DIM), 0.02),
        'dt_bias_f': dt_f0 + jnp.log(-jnp.expm1(-dt_f0)),
        'dt_bias_b': dt_b0 + jnp.log(-jnp.expm1(-dt_b0)),
        'a_log_f': jnp.log(jax.random.uniform(ks[12], (L, SSD_HEADS), jnp.float32, 1.0, 16.0)),
        'a_log_b': jnp.log(jax.random.uniform(ks[13], (L, SSD_HEADS), jnp.float32, 1.0, 16.0)),
        'd_skip': gain(ks[14], (L, SSD_HEADS)),
        'ssd_norm_g': gain(ks[15], (L, D_SSD)),
        'gmlp_ln_g': gain(ks[16], (L, D_GMLP)),
        'gmlp_ln_b': nrm(ks[17], (L, D_GMLP), 0.02),
        'spatial_w': nrm(ks[18], (L, GMLP_GROUPS, CHUNK, CHUNK), CHUNK ** -0.5),
        'spatial_b': gain(ks[19], (L, GMLP_GROUPS, CHUNK)),
        'w_out': nrm(ks[20], (L, D_MIX, D_MODEL), D_MIX ** -0.5),
        'mix_post_g': gain(ks[21], (L, D_MODEL)),
        'ff2_pre_g': gain(ks[22], (L, D_MODEL)),
        'ff2_w_gate': nrm(ks[23], (L, D_MODEL, D_FF), D_MODEL ** -0.5),
        'ff2_w_up': nrm(ks[24], (L, D_MODEL, D_FF), D_MODEL ** -0.5),
        'ff2_w_down': nrm(ks[25], (L, D_FF, D_MODEL), D_FF ** -0.5),
        'ff2_post_g': gain(ks[26], (L, D_MODEL)),
    }


def reference(x, ff1_pre_g, ff1_w_gate, ff1_w_up, ff1_w_down, ff1_post_g,
              mix_pre_g, w_in, conv_w, conv_b, dt_bias_f, dt_bias_b, a_log_f, a_log_b,
              d_skip, ssd_norm_g, gmlp_ln_g, gmlp_ln_b, spatial_w, spatial_b, w_out,
              mix_post_g, ff2_pre_g, ff2_w_gate, ff2_w_up, ff2_w_down, ff2_post_g):
    for l in range(DEPTH):
        f = swiglu(rms_norm(x, ff1_pre_g[l]), ff1_w_gate[l], ff1_w_up[l], ff1_w_down[l])
        x = x + FFN_RESIDUAL_WEIGHT * rms_norm(f, ff1_post_g[l])
        m = token_mix(rms_norm(x, mix_pre_g[l]), w_in[l], conv_w[l], conv_b[l],
                      dt_bias_f[l], dt_bias_b[l], a_log_f[l], a_log_b[l], d_skip[l],
                      ssd_norm_g[l], gmlp_ln_g[l], gmlp_ln_b[l], spatial_w[l], spatial_b[l], w_out[l])
        x = x + rms_norm(m, mix_post_g[l])
        f = swiglu(rms_norm(x, ff2_pre_g[l]), ff2_w_gate[l], ff2_w_up[l], ff2_w_down[l])
        x = x + FFN_RESIDUAL_WEIGHT * rms_norm(f, ff2_post_g[l])
    return x
```

```python
import numpy as np
import concourse.bass as bass
import concourse.mybir as mybir
from contextlib import ExitStack

F32 = mybir.dt.float32
BF16 = mybir.dt.bfloat16
AF = mybir.ActivationFunctionType
ALU = mybir.AluOpType
AX = mybir.AxisListType

ENGS = ("pe", "act", "dve", "pool", "sp")
_UIDC = [0]


def _uid():
    _UIDC[0] += 1
    return _UIDC[0]
DMA_INC = 16


class T:
    __slots__ = ("name", "w", "r")

    def __init__(self, name=""):
        self.name = name
        self.w = None
        self.r = {}


class EngQ:
    def __init__(self, name, nsem, is_dma=False):
        self.name = name
        self.nsem = nsem
        self.is_dma = is_dma
        self.sems = []
        self.count = 0
        self.stream = []
        self.seen = {}


class Prog:
    def __init__(self, nc, nsem_compute=4, nsem_dma=12):
        self.nc = nc
        self.q = {e: EngQ(e, nsem_compute) for e in ENGS}
        for e in ("sp", "act", "pool"):
            self.q["d" + e] = EngQ("d" + e, 6 if e == "pool" else nsem_dma, is_dma=True)
        self.n_ops = 0

    def _semval(self, q, iid):
        k = q.nsem
        idx = (iid - 1) % k
        val = (iid - 1) // k + 1
        if q.is_dma:
            val *= DMA_INC
        return idx, val

    def op(self, eng, fn, reads=(), writes=(), signal=True, dma=False):
        q = self.q[eng]
        need = {}

        def add(dep):
            if dep is None:
                return
            e, i = dep
            qq = self.q[e]
            key = (e, (i - 1) % qq.nsem) if qq.is_dma else e
            if need.get(key, (None, 0))[1] < i:
                need[key] = (e, i)

        for t in reads:
            add(t.w)
        for t in writes:
            add(t.w)
            for d in t.r.values():
                add(d)
        qsig = self.q["d" + eng] if dma else q
        if dma:
            if not signal:
                raise RuntimeError("dma must signal")
            nxt = qsig.count + 1
            if nxt > qsig.nsem:
                add((qsig.name, nxt - qsig.nsem))
        waits = []
        for key, (e, i) in need.items():
            if e == "pe" and eng == "pe" and not dma:
                continue
            if q.seen.get(key, 0) >= i:
                continue
            if i > self.q[e].count:
                raise RuntimeError(f"dependency on future/unsignalled instr {(e, i)} (count {self.q[e].count})")
            q.seen[key] = i
            idx, val = self._semval(self.q[e], i)
            waits.append((e, idx, val))
        if signal:
            qsig.count += 1
            iid = qsig.count
            sidx, _ = self._semval(qsig, iid)
            sig = (qsig.name, sidx, DMA_INC if dma else 1)
            myid = (qsig.name, iid)
        else:
            sig = None
            myid = (qsig.name, qsig.count + 1)
        q.stream.append((waits, fn, sig))
        self.n_ops += 1
        qn, qi = myid
        rkey = (qn, (qi - 1) % qsig.nsem) if qsig.is_dma else qn
        for t in reads:
            old = t.r.get(rkey)
            if old is None or old[1] < qi:
                t.r[rkey] = myid
        for t in writes:
            t.w = myid
            t.r = {}
        return myid

    def build(self, final_waits=()):
        nc = self.nc
        if final_waits:
            self.op("sp", None, reads=list(final_waits), signal=False)
        with ExitStack() as es:
            for name, q in self.q.items():
                if q.count == 0:
                    continue
                q.sems = [es.enter_context(nc.semaphore(f"s_{name}_{i}")) for i in range(q.nsem)]
            block = es.enter_context(nc.Block())
            handles = {"pe": block.tensor, "act": block.scalar, "dve": block.vector,
                       "pool": block.gpsimd, "sp": block.sync}
            for name in ENGS:
                q = self.q[name]
                if not q.stream:
                    continue

                def body(e, q=q):
                    for waits, fn, sig in q.stream:
                        for (en, idx, val) in waits:
                            e.wait_ge(self.q[en].sems[idx], val)
                        if fn is None:
                            continue
                        ins = fn(e)
                        if sig is not None:
                            ins.then_inc(self.q[sig[0]].sems[sig[1]], sig[2])
                handles[name](body)


def barrier(P):
    targets = []
    for name, qq in P.q.items():
        if qq.count == 0:
            continue
        if qq.is_dma:
            for i in range(max(1, qq.count - qq.nsem + 1), qq.count + 1):
                targets.append((name, i))
        else:
            targets.append((name, qq.count))
    for e in ENGS:
        q = P.q[e]
        waits = []
        for (n, i) in targets:
            qq = P.q[n]
            if n == e and e == "pe":
                continue
            key = (n, (i - 1) % qq.nsem) if qq.is_dma else n
            if q.seen.get(key, 0) >= i:
                continue
            q.seen[key] = i
            idx, val = P._semval(qq, i)
            waits.append((n, idx, val))
        if waits:
            q.stream.append((waits, None, None))


class Banks:
    def __init__(self, nc, es):
        self.t = [es.enter_context(nc.psum_tensor(f"psb{i}", [128, 512], F32)) for i in range(8)]
        self.T = [T(f"psb{i}") for i in range(8)]
        self.rot = list(range(8))
        self.i = 0

    def get(self):
        i = self.rot[self.i % len(self.rot)]
        self.i += 1
        return self.t[i], self.T[i]

    def reserve(self, n):
        r = self.rot[-n:]
        self.rot = self.rot[:-n]
        return [(self.t[i], self.T[i]) for i in r]

    def release(self):
        self.rot = list(range(8))


D = 1024
L = 4096
DFF = 2816
NM = DFF // 128
NK = D // 128
EPS = 1e-6
TT = 512
NT = L // TT


def rstd_from_ss(P, ss_ps, ssT, rstd, rT, n):
    P.op("act", lambda e: e.activation(out=rstd, in_=ss_ps, func=AF.Ln, scale=1.0 / n, bias=EPS), reads=[ssT], writes=[rT])
    P.op("act", lambda e: e.activation(out=rstd, in_=rstd, func=AF.Exp, scale=-0.5), reads=[rT], writes=[rT])


def ffn_stage(P, nc, banks, src, dst, wg_d, wu_d, wd_d, gpre_d, gpost_d, wT, ones_bf, onesT):
    srcv = src.rearrange("(k p) t -> p k t", p=128)
    dstv = dst.rearrange("(k p) t -> p k t", p=128)
    with ExitStack() as es:
        sb = lambda name, shape, dt: es.enter_context(nc.sbuf_tensor(f"{name}_{_uid()}", shape, dt))
        xt = [sb(f"f_xt{i}", [128, NK, TT], F32) for i in range(2)]
        xtT = [T() for _ in range(2)]
        sq = sb("f_sq", [128, NK, TT], BF16); sqT = T()
        h = [sb(f"f_h{i}", [128, NK, TT], BF16) for i in range(2)]
        hT = [T() for _ in range(2)]
        a = sb("f_a", [128, NM, TT], BF16)
        aT = [T() for _ in range(NM)]
        wd = sb("f_wd", [128, NM, D], BF16); wdT = T()
        NSL = 3
        wgs = [sb(f"f_wg{i}", [128, 2, 1024], BF16) for i in range(NSL)]
        wus = [sb(f"f_wu{i}", [128, 2, 1024], BF16) for i in range(NSL)]
        wgT = [T() for _ in range(NSL)]
        wuT = [T() for _ in range(NSL)]
        sg = [sb(f"f_sg{i}", [128, TT], BF16) for i in range(2)]
        sgT = [T() for _ in range(2)]
        fsb = sb("f_fsb", [128, NK, TT], F32); fsbT = [T() for _ in range(NK)]
        sq2 = sq; sq2T = [sqT for _ in range(NK)]
        rstd = [sb(f"f_rstd{i}", [128, TT], F32) for i in range(2)]
        rstdT = [T() for _ in range(2)]
        rstd2 = sb("f_rstd2", [128, TT], F32); rstd2T = T()
        tmp = [sb(f"f_tmp{i}", [128, TT], F32) for i in range(2)]
        tmpT = [T() for _ in range(2)]
        gpre = sb("f_gpre", [128, NK], F32); gpost = sb("f_gpost", [128, NK], F32)
        gT = T()

        P.op("sp", lambda e: e.dma_start(out=gpre[:], in_=gpre_d), writes=[gT], dma=True)
        P.op("sp", lambda e: e.dma_start(out=gpost[:], in_=gpost_d), writes=[gT], dma=True)
        P.op("dve", lambda e: e.tensor_scalar(out=gpost[:], in0=gpost[:], scalar1=0.5, scalar2=None, op0=ALU.mult),
             reads=[gT], writes=[gT])
        def load_wd():
            for q4 in range(0, NM, 6):
                n = min(6, NM - q4)
                P.op("sp", lambda e, q4=q4, n=n: e.dma_start(out=wd[:, q4:q4 + n, :],
                                                           in_=wd_d[q4:q4 + n].rearrange("m p n -> p m n")),
                     reads=wT, writes=[wdT], dma=True)

        def xload(t):
            b = t % 2
            P.op("sp", lambda e: e.dma_start(out=xt[b][:], in_=srcv[:, :, t * TT:(t + 1) * TT]),
                 writes=[xtT[b]], dma=True)

        def prologue(t):
            b = t % 2
            P.op("act", lambda e: e.activation(out=sq[:], in_=xt[b][:], func=AF.Square), reads=[xtT[b]], writes=[sqT])
            ps, psT = banks.get()
            for k in range(NK):
                P.op("pe", lambda e, k=k: e.matmul(ps[:], lhsT=ones_bf, rhs=sq[:, k, :], start=(k == 0), stop=(k == NK - 1)),
                     reads=[sqT, onesT], writes=[psT], signal=(k == NK - 1))
            rstd_from_ss(P, ps[:], psT, rstd[b][:], rstdT[b], D)
            for k in range(NK):
                P.op("dve", lambda e, k=k: e.scalar_tensor_tensor(out=h[b][:, k, :], in0=xt[b][:, k, :], scalar=gpre[:, k:k + 1],
                                                                  in1=rstd[b][:], op0=ALU.mult, op1=ALU.mult),
                     reads=[xtT[b], rstdT[b], gT], writes=[hT[b]])

        slab_ctr = [0]
        outs = []

        def gateup(t):
            b = t % 2
            for mp in range(NM // 2):
                s = slab_ctr[0] % NSL
                slab_ctr[0] += 1
                P.op("sp", lambda e, s=s, mp=mp: e.dma_start(out=wgs[s][:], in_=wg_d[2 * mp:2 * mp + 2].rearrange("m p f -> p m f")),
                     reads=wT, writes=[wgT[s]], dma=True)
                P.op("sp", lambda e, s=s, mp=mp: e.dma_start(out=wus[s][:], in_=wu_d[2 * mp:2 * mp + 2].rearrange("m p f -> p m f")),
                     reads=wT, writes=[wuT[s]], dma=True)
                if t == 0 and mp == 2:
                    load_wd()
                if mp == 5 and t + 1 < NT:
                    xload(t + 1)
                for mi in range(2):
                    m = 2 * mp + mi
                    pg, pgT = banks.get()
                    pu, puT = banks.get()
                    for k in range(NK):
                        P.op("pe", lambda e, k=k, s=s, mi=mi, pg=pg: e.matmul(pg[:], lhsT=wgs[s][:, mi, k * 128:(k + 1) * 128], rhs=h[b][:, k, :],
                                                                           start=(k == 0), stop=(k == NK - 1)),
                             reads=[wgT[s], hT[b]], writes=[pgT], signal=(k == NK - 1))
                    for k in range(NK):
                        P.op("pe", lambda e, k=k, s=s, mi=mi, pu=pu: e.matmul(pu[:], lhsT=wus[s][:, mi, k * 128:(k + 1) * 128], rhs=h[b][:, k, :],
                                                                           start=(k == 0), stop=(k == NK - 1)),
                             reads=[wuT[s], hT[b]], writes=[puT], signal=(k == NK - 1))
                    j = m % 2
                    P.op("act", lambda e, j=j, pg=pg: e.activation(out=sg[j][:], in_=pg[:], func=AF.Silu), reads=[pgT], writes=[sgT[j]])
                    P.op("dve", lambda e, j=j, pu=pu, m=m: e.tensor_tensor(out=a[:, m, :], in0=pu[:], in1=sg[j][:], op=ALU.mult),
                         reads=[puT, sgT[j]], writes=[aT[m]])

        def down(t):
            b = t % 2
            for d in range(NK):
                pf, pfT = banks.get()
                for m in range(NM):
                    P.op("pe", lambda e, m=m, d=d, pf=pf: e.matmul(pf[:], lhsT=wd[:, m, d * 128:(d + 1) * 128], rhs=a[:, m, :],
                                                               start=(m == 0), stop=(m == NM - 1)),
                         reads=[wdT, aT[m]], writes=[pfT], signal=(m == NM - 1))
                P.op("act", lambda e, d=d, pf=pf: e.activation(out=fsb[:, d, :], in_=pf[:], func=AF.Copy), reads=[pfT], writes=[fsbT[d]])
                P.op("act", lambda e, d=d, pf=pf: e.activation(out=sq2[:, d, :], in_=pf[:], func=AF.Square), reads=[pfT], writes=[sq2T[d]])
            ps, psT = banks.get()
            for k in range(NK):
                P.op("pe", lambda e, k=k: e.matmul(ps[:], lhsT=ones_bf, rhs=sq2[:, k, :], start=(k == 0), stop=(k == NK - 1)),
                     reads=[sq2T[k], onesT], writes=[psT], signal=(k == NK - 1))
            rstd_from_ss(P, ps[:], psT, rstd2[:], rstd2T, D)
            for d in range(NK):
                j = d % 2
                P.op("dve", lambda e, d=d, j=j: e.scalar_tensor_tensor(out=tmp[j][:], in0=fsb[:, d, :], scalar=gpost[:, d:d + 1], in1=rstd2[:],
                                                                    op0=ALU.mult, op1=ALU.mult),
                     reads=[fsbT[d], rstd2T, gT], writes=[tmpT[j]])
                P.op("pool", lambda e, d=d, j=j: e.tensor_tensor(out=fsb[:, d, :], in0=tmp[j][:], in1=xt[b][:, d, :], op=ALU.add),
                     reads=[tmpT[j], xtT[b]], writes=[fsbT[d]])
            oT = T()
            outs.append(oT)
            P.op("act", lambda e: e.dma_start(out=dstv[:, :, t * TT:(t + 1) * TT], in_=fsb[:]), reads=fsbT, writes=[oT], dma=True)

        xload(0)
        prologue(0)
        for t in range(NT):
            gateup(t)
            run_hooks(3)
            if t + 1 < NT:
                prologue(t + 1)
            down(t)
        barrier(P)
        return outs


NCH = L // 128
NCH_RUN = NCH
DIN = 5152
C_Z, C_X, C_DT, C_U, C_V = 0, 1024, 3072, 3104, 4128


def load_consts(P, nc, es, consts_d):
    sb = lambda name, shape, dt: es.enter_context(nc.sbuf_tensor(f"{name}_{_uid()}", shape, dt))
    cf = sb("c_f32", [128, 6, 128], F32)
    cb = sb("c_bf", [128, 6, 128], BF16)
    cT = T("consts")
    P.op("sp", lambda e: e.dma_start(out=cf[:], in_=consts_d), writes=[cT], dma=True)
    P.op("dve", lambda e: e.tensor_copy(out=cb[:], in_=cf[:]), reads=[cT], writes=[cT])
    return dict(cf=cf, cb=cb, T=cT, ident_bf=cb[:, 0, :], ones_bf=cb[:, 5, :], ones_f=cf[:, 5, :],
                triF=cf[:, 1, :], triB=cf[:, 2, :], Af=cf[:, 3, :], Ab=cf[:, 4, :],
                triF_bf=cb[:, 1, :], triB_bf=cb[:, 2, :])


def mixer_stage(P, nc, banks, src, dst, prm, scr, C):
    cT = C["T"]
    srcv = src.rearrange("(k p) t -> p k t", p=128)
    dstv = dst.rearrange("(k p) t -> p k t", p=128)
    YP, ZS, YG, CTs, SM, STB = scr["YP"], scr["ZS"], scr["YG"], scr["CT"], scr["SM"], scr["STB"]
    outs = []
    def sweepA():
        with ExitStack() as es:
            sb = lambda name, shape, dt: es.enter_context(nc.sbuf_tensor(f"{name}_{_uid()}", shape, dt))
            OP = P.op
            win = sb("a_win", [128, NK, DIN], BF16)
            wblocks = [(C_DT, C_DT + 32), (C_X, C_X + 1024), (C_X + 1024, C_X + 2048), (C_V, C_V + 1024), (C_Z, C_Z + 1024), (C_U, C_U + 1024)]
            wbT = [T() for _ in wblocks]
            win_v = prm["win_b"].rearrange("(k p) n -> p k n", p=128)
            for (c0, c1), tr in zip(wblocks, wbT):
                OP("sp", lambda e, c0=c0, c1=c1: e.dma_start(out=win[:, :, c0:c1], in_=win_v[:, :, c0:c1]), reads=prm["wT"], writes=[tr], dma=True)

            def winT_of(col):
                for (c0, c1), tr in zip(wblocks, wbT):
                    if c0 <= col < c1:
                        return tr
                raise RuntimeError("col")
            prT = T("params")
            prLs = []
            convw = sb("a_convw", [128, 16, 5], F32); convb = sb("a_convb", [128, 16], F32)
            dtbias = sb("a_dtb", [128, 32], F32); abc = sb("a_abc", [128, 32], F32)
            dsk = sb("a_dsk", [128, 8], F32); gpre = sb("a_gpre", [128, 8], F32)
            lng = sb("a_lng", [128, 1024], F32); lnb = sb("a_lnb", [128, 1024], F32)
            swf = sb("a_swf", [128, 8, 128], F32); swb = sb("a_swb", [128, 8, 128], BF16)
            sbrow = sb("a_sbrow", [1, 1024], F32)
            for t_, d_ in ((convw, "convw"), (convb, "convb"), (dtbias, "dtbias"), (abc, "alog"), (dsk, "dsk"), (gpre, "gpre"),
                           (lng, "lng"), (lnb, "lnb"), (swf, "swT"), (sbrow, "sbrow")):
                prLs.append(T())
                OP("sp", lambda e, t_=t_, d_=d_: e.dma_start(out=t_[:], in_=prm[d_]), writes=[prLs[-1]], dma=True)
            OP("act", lambda e: e.activation(out=abc[:], in_=abc[:], func=AF.Exp), reads=prLs, writes=[prT])
            OP("dve", lambda e: e.tensor_scalar(out=abc[:], in0=abc[:], scalar1=-1.0, scalar2=None, op0=ALU.mult), reads=[prT], writes=[prT])
            OP("dve", lambda e: e.tensor_copy(out=swb[:], in_=swf[:]), reads=[prT], writes=[prT])
            cdiag = sb("a_cdiag", [128, 16, 5, 128], BF16)
            for q in range(16):
                for tp in range(5):
                    OP("dve", lambda e, q=q, tp=tp: e.tensor_scalar(out=cdiag[:, q, tp, :], in0=C["ident_bf"], scalar1=convw[:, q, tp:tp + 1],
                                                                  scalar2=None, op0=ALU.mult), reads=[prT, cT], writes=[prT])

            xh = sb("a_xh", [128, NK, 132], F32); xhT = T()
            sq = sb("a_sq", [128, NK, 132], BF16); sqT = T()
            rstd = sb("a_rstd", [128, 132], F32); rstdT = T()
            gT_ = [sb(f"a_gT{i}", [128, NK, 132], BF16) for i in range(2)]; gTT = [T() for _ in range(2)]
            xbc = sb("a_xbc", [128, 16, 132], BF16); xbcT = [T() for _ in range(6)]
            xc = [sb(f"a_xc{i}", [128, 16, 128], BF16) for i in range(2)]; xcT = [[T() for _ in range(4)] for _ in range(2)]
            zs = sb("a_zs", [128, 8, 128], BF16); zsT = T()
            ug = sb("a_ug", [128, 8, 128], BF16); ugT = T()
            vg = sb("a_vg", [128, 1024], F32); vgT = T()
            vb = sb("a_vb", [128, 1024], BF16); vbT = T()
            ygm = sb("a_ygm", [128, 8, 128], BF16); ygmT = T()
            bst = sb("a_bst", [128, 2, 6], F32); mv = sb("a_mv", [128, 2], F32); rsv = sb("a_rsv", [128, 1], F32); lnT = T()
            smalls = [{n: sb(f"a_{n}{i}", [128, 32], F32) for n in ("dtr", "dt", "dta", "cs_sb", "ecs", "dec", "sm", "etot", "dtd")} for i in range(2)]
            smallTs = [T() for _ in range(2)]; smTs = [T() for _ in range(2)]
            xdt = [sb(f"a_xdt{i}", [128, 1024], BF16) for i in range(4)]; xdtT = [T() for _ in range(4)]
            btok = sb("a_btok", [128, 512], BF16); btokT = T()
            cbm = [sb(f"a_cbm{i}", [128, 512], BF16) for i in range(2)]; cbmT = [T() for _ in range(2)]
            dtri2 = [sb(f"a_dtri2{i}", [128, 2, 4, 128], F32) for i in range(2)]; dtri2T = [T() for _ in range(2)]
            Eb = [sb(f"a_E{i}", [128, 512], BF16) for i in range(2)]; EbT = [T() for _ in range(2)]
            Mb = [sb(f"a_M{i}", [128, 512], BF16) for i in range(4)]; MbT = [T() for _ in range(4)]
            yoff = sb("a_yoff", [128, 1024], BF16); yoffT = T()
            Sf = sb("a_Sf", [128, 1024], F32); SfT = T()
            Sbf = sb("a_Sbf", [128, 1024], BF16); SbfT = T()
            ypart = sb("a_ypart", [128, 8, 128], F32); ypT = T()
            stb = sb("a_stb", [128, 1024], F32); stbT = T()
            stf = sb("a_stf", [128, 1024], F32); stfT = T()
            OP("dve", lambda e: e.memset(Sf[:], 0.0), writes=[SfT])
            OP("dve", lambda e: e.memset(Sbf[:], 0.0), writes=[SbfT])

            psy = banks.reserve(2)
            segbanks = banks.reserve(3)
            rot_base = list(banks.rot)
            seg_ids = [i for i in range(8) if any(banks.t[i] is sbk[0] for sbk in segbanks)]

            def norm_a(c):
                lo, hi = 128 * c - 2, 128 * c + 130
                if c == 0:
                    OP("dve", lambda e: e.memset(xh[:, :, 0:2], 0.0), writes=[xhT])
                    OP("sp", lambda e: e.dma_start(out=xh[:, :, 2:132], in_=srcv[:, :, 0:130]), writes=[xhT], dma=True)
                elif c == NCH - 1:
                    OP("dve", lambda e: e.memset(xh[:, :, 130:132], 0.0), writes=[xhT])
                    OP("sp", lambda e, lo=lo: e.dma_start(out=xh[:, :, 0:130], in_=srcv[:, :, lo:L]), writes=[xhT], dma=True)
                else:
                    OP("sp", lambda e, lo=lo, hi=hi: e.dma_start(out=xh[:, :, :], in_=srcv[:, :, lo:hi]), writes=[xhT], dma=True)
                OP("act", lambda e: e.activation(out=sq[:], in_=xh[:], func=AF.Square), reads=[xhT], writes=[sqT])

            def norm_b(c):
                b = c % 2
                g_ = gT_[b]; gt = gTT[b]
                ps, psT = banks.get()
                for k in range(NK):
                    OP("pe", lambda e, k=k, ps=ps: e.matmul(ps[:, 0:132], lhsT=C["ones_bf"], rhs=sq[:, k, :], start=(k == 0), stop=(k == NK - 1)),
                       reads=[sqT, cT], writes=[psT], signal=(k == NK - 1))
                rstd_from_ss(P, ps[:, 0:132], psT, rstd[:], rstdT, D)
                for k in range(NK):
                    OP("dve", lambda e, k=k, g_=g_: e.scalar_tensor_tensor(out=g_[:, k, :], in0=xh[:, k, :], scalar=gpre[:, k:k + 1], in1=rstd[:],
                                                                        op0=ALU.mult, op1=ALU.mult), reads=[xhT, rstdT, prT], writes=[gt])

            def front(c):
                b = c % 2
                g_ = gT_[b]; gt = gTT[b]; xcc = xc[b]; xct = xcT[b]
                sm_ = smalls[b]; smallT = smallTs[b]; smT = smTs[b]
                dtr, dt, dta, cs_sb, ecs, dec, sm, etot, dtd = (sm_[n] for n in ("dtr", "dt", "dta", "cs_sb", "ecs", "dec", "sm", "etot", "dtd"))

                def proj(psl, col, n0, n1, last, g_=g_, gt=gt):
                    for k in range(NK):
                        OP("pe", lambda e, k=k: e.matmul(psl, lhsT=win[:, k, col:col + 128], rhs=g_[:, k, n0:n1], start=(k == 0), stop=(k == NK - 1)),
                           reads=[winT_of(col), gt], writes=[last[1]], signal=(k == NK - 1 and last[0]))

                ps, psT = banks.get()
                for k in range(NK):
                    OP("pe", lambda e, k=k, ps=ps, g_=g_: e.matmul(ps[:, 0:32], lhsT=g_[:, k, 2:130], rhs=win[:, k, C_DT:C_DT + 32], start=(k == 0), stop=(k == NK - 1)),
                       reads=[winT_of(C_DT), gt], writes=[psT], signal=(k == NK - 1))
                OP("dve", lambda e, ps=ps: e.tensor_tensor(out=dtr[:], in0=ps[:, 0:32], in1=dtbias[:], op=ALU.add), reads=[psT, prT], writes=[smallT])
                OP("act", lambda e: e.activation(out=dtr[:], in_=dtr[:], func=AF.Exp), reads=[smallT], writes=[smallT])
                OP("act", lambda e: e.activation(out=dt[:], in_=dtr[:], func=AF.Ln, bias=1.0), reads=[smallT], writes=[smallT])
                OP("dve", lambda e: e.tensor_tensor(out=dta[:], in0=dt[:], in1=abc[:], op=ALU.mult), reads=[smallT, prT], writes=[smallT])
                yield
                for bi in range(6):
                    qs = list(range(3 * bi, min(16, 3 * bi + 3)))
                    ps, psT = banks.get()
                    for j, q in enumerate(qs):
                        proj(ps[:, j * 132:(j + 1) * 132], C_X + 128 * q, 0, 132, (j == len(qs) - 1, psT))
                    n = len(qs)
                    OP("act", lambda e, ps=ps, q0=qs[0], n=n: e.activation(out=xbc[:, q0:q0 + n, :], in_=ps[:, 0:n * 132].rearrange("p (q t) -> p q t", q=n),
                                                                      func=AF.Copy), reads=[psT], writes=[xbcT[bi]])
                    if bi == 0 and c + 1 < NCH:
                        norm_a(c + 1)
                    if bi == 4 and c + 1 < NCH:
                        norm_b(c + 1)
                    yield
                yield "P"
                for hf in range(2):
                    ps, psT = banks.get()
                    for k in range(NK):
                        OP("pe", lambda e, k=k, ps=ps, hf=hf, g_=g_: e.matmul(ps[:], lhsT=g_[:, k, 2:130], rhs=win[:, k, C_V + 512 * hf:C_V + 512 * hf + 512],
                                                                          start=(k == 0), stop=(k == NK - 1)),
                           reads=[winT_of(C_V), gt], writes=[psT], signal=(k == NK - 1))
                    OP("act", lambda e, ps=ps, hf=hf: e.activation(out=vg[:, 512 * hf:512 * hf + 512], in_=ps[:], func=AF.Gelu), reads=[psT], writes=[vgT])
                    OP("dve", lambda e, hf=hf: e.bn_stats(out=bst[:, hf, :], in_=vg[:, 512 * hf:512 * hf + 512]), reads=[vgT], writes=[lnT])
                    yield
                OP("dve", lambda e: e.bn_aggr(out=mv[:], in_=bst[:].rearrange("p a b -> p (a b)")), reads=[lnT], writes=[lnT])
                OP("act", lambda e: e.activation(out=rsv[:], in_=mv[:, 1:2], func=AF.Ln, bias=EPS), reads=[lnT], writes=[lnT])
                OP("act", lambda e: e.activation(out=rsv[:], in_=rsv[:], func=AF.Exp, scale=-0.5), reads=[lnT], writes=[lnT])
                OP("dve", lambda e: e.tensor_scalar(out=vg[:], in0=vg[:], scalar1=mv[:, 0:1], scalar2=rsv[:, 0:1], op0=ALU.subtract, op1=ALU.mult),
                   reads=[vgT, lnT], writes=[vgT])
                OP("dve", lambda e: e.tensor_tensor(out=vg[:], in0=vg[:], in1=lng[:], op=ALU.mult), reads=[vgT, prT], writes=[vgT])
                OP("dve", lambda e: e.tensor_tensor(out=vb[:], in0=vg[:], in1=lnb[:], op=ALU.add), reads=[vgT, prT], writes=[vbT])
                yield
                for bi in range(4):
                    ps, psT = banks.get()
                    for j in range(4):
                        q = 4 * bi + j
                        for tp in range(5):
                            OP("pe", lambda e, q=q, tp=tp, j=j, ps=ps: e.matmul(ps[:, j * 128:(j + 1) * 128], lhsT=cdiag[:, q, tp, :], rhs=xbc[:, q, tp:tp + 128],
                                                                              start=(tp == 0), stop=(tp == 4)),
                               reads=[prT, xbcT[q // 3]], writes=[psT], signal=(tp == 4 and j == 3))
                    for j in range(4):
                        q = 4 * bi + j
                        OP("act", lambda e, q=q, j=j, ps=ps, xcc=xcc: e.activation(out=xcc[:, q, :], in_=ps[:, j * 128:(j + 1) * 128], func=AF.Silu,
                                                                               bias=convb[:, q:q + 1]), reads=[psT, prT], writes=[xct[bi]])
                    yield
                yield "P"
                ps, psT = banks.get()
                OP("pe", lambda e, ps=ps: e.matmul(ps[:, 0:16], lhsT=C["triF"], rhs=dta[:, 0:16], start=True, stop=True), reads=[cT, smallT], writes=[psT], signal=False)
                OP("pe", lambda e, ps=ps: e.matmul(ps[:, 16:32], lhsT=C["triB"], rhs=dta[:, 16:32], start=True, stop=True), reads=[cT, smallT], writes=[psT], signal=False)
                OP("pe", lambda e, ps=ps: e.matmul(ps[:, 32:64], lhsT=C["ones_f"], rhs=dta[:, 0:32], start=True, stop=True), reads=[cT, smallT], writes=[psT])
                OP("act", lambda e, ps=ps: e.activation(out=cs_sb[:], in_=ps[:, 0:32], func=AF.Copy), reads=[psT], writes=[smallT])
                OP("act", lambda e, ps=ps: e.activation(out=ecs[:], in_=ps[:, 0:32], func=AF.Exp), reads=[psT], writes=[smallT])
                OP("act", lambda e, ps=ps: e.activation(out=etot[:], in_=ps[:, 32:64], func=AF.Exp), reads=[psT], writes=[smallT])
                OP("dve", lambda e, ps=ps: e.tensor_tensor(out=dec[:], in0=ps[:, 32:64], in1=cs_sb[:], op=ALU.subtract), reads=[psT, smallT], writes=[smallT])
                OP("act", lambda e: e.activation(out=dec[:], in_=dec[:], func=AF.Exp), reads=[smallT], writes=[smallT])
                OP("dve", lambda e: e.tensor_tensor(out=dtd[:], in0=dt[:], in1=dec[:], op=ALU.mult), reads=[smallT], writes=[smallT])
                OP("dve", lambda e: e.tensor_copy(out=sm[:, 0:16], in_=ecs[:, 16:32]), reads=[smallT], writes=[smT])
                OP("dve", lambda e: e.tensor_copy(out=sm[:, 16:32], in_=etot[:, 16:32]), reads=[smallT], writes=[smT])
                yield
                for (col0, dstt, dT, fn) in ((C_Z, zs, zsT, AF.Silu), (C_U, ug, ugT, AF.Gelu)):
                    for bi in range(2):
                        ps, psT = banks.get()
                        for j in range(4):
                            proj(ps[:, j * 128:(j + 1) * 128], col0 + 128 * (4 * bi + j), 2, 130, (j == 3, psT))
                        OP("act", lambda e, ps=ps, bi=bi, dstt=dstt, fn=fn: e.activation(out=dstt[:, 4 * bi:4 * bi + 4, :],
                                                                                   in_=ps[:].rearrange("p (q t) -> p q t", q=4), func=fn),
                           reads=[psT], writes=[dT])
                        yield
                for bi in range(2):
                    ps, psT = banks.get()
                    for j in range(4):
                        g = 4 * bi + j
                        OP("pe", lambda e, g=g, j=j, ps=ps: e.matmul(ps[:, j * 128:(j + 1) * 128], lhsT=vb[:, g * 128:(g + 1) * 128], rhs=swb[:, g, :], start=True, stop=False),
                           reads=[vbT, prT], writes=[psT], signal=False)
                        OP("pe", lambda e, g=g, j=j, ps=ps: e.matmul(ps[:, j * 128:(j + 1) * 128], lhsT=C["ones_f"][0:1, :], rhs=sbrow[0:1, g * 128:(g + 1) * 128], start=False, stop=True),
                           reads=[cT, prT], writes=[psT], signal=(j == 3))
                    OP("dve", lambda e, bi=bi, ps=ps: e.tensor_tensor(out=ygm[:, 4 * bi:4 * bi + 4, :], in0=ps[:].rearrange("p (q t) -> p q t", q=4),
                                                                  in1=ug[:, 4 * bi:4 * bi + 4, :], op=ALU.mult), reads=[psT, ugT], writes=[ygmT])
                    yield
                OP("sp", lambda e, c=c: e.dma_start(out=ZS[c], in_=zs[:].rearrange("p q t -> p (q t)")), reads=[zsT], writes=[T()], dma=True)
                OP("sp", lambda e, c=c: e.dma_start(out=YG[c], in_=ygm[:].rearrange("p q t -> p (q t)")), reads=[ygmT], writes=[T()], dma=True)
                OP("sp", lambda e, c=c: e.dma_start(out=SM[c], in_=sm[:]), reads=[smT], writes=[T()], dma=True)
                yield

            def back(c):
                b = c % 2
                xcc = xc[b]; xct = xcT[b]
                sm_ = smalls[b]; smallT = smallTs[b]; smT = smTs[b]
                dtr, dt, dta, cs_sb, ecs, dec, sm, etot, dtd = (sm_[n] for n in ("dtr", "dt", "dta", "cs_sb", "ecs", "dec", "sm", "etot", "dtd"))
                dirs = ((C["Af"], C["triF"], 0), (C["Ab"], C["triB"], 16))

                def dtri_issue(g):
                    gb = g % 2
                    OP("dve", lambda e, gb=gb, g=g: e.tensor_tensor(out=dtri2[gb][:], in0=C["cf"][:, 1:3, :].unsqueeze(2).to_broadcast([128, 2, 4, 128]),
                                                                 in1=dta[:].rearrange("p (d h) -> p d h", d=2)[:, :, 4 * g:4 * g + 4].unsqueeze(3).to_broadcast([128, 2, 4, 128]),
                                                                 op=ALU.mult),
                       reads=[cT, smallT], writes=[dtri2T[gb]])

                dtri_issue(0)
                dtri_issue(1)
                ps, psT = banks.get()
                psb = ps[:].bitcast(BF16)
                for q in range(8):
                    OP("pe", lambda e, q=q, psb=psb, xcc=xcc: e.transpose(out=psb[:, q * 128:(q + 1) * 128], in_=xcc[:, q, :], identity=C["ident_bf"]),
                       reads=[xct[q // 4], cT], writes=[psT], signal=(q == 7))
                for i, scl in enumerate((dt[:, 0:16], dt[:, 16:32], dtd[:, 0:16], dtd[:, 16:32])):
                    OP("dve", lambda e, i=i, scl=scl, psb=psb: e.tensor_tensor(out=xdt[i][:].rearrange("p (h d) -> p h d", h=16), in0=psb.rearrange("p (h d) -> p h d", h=16),
                                                                             in1=scl.unsqueeze(2).to_broadcast([128, 16, 64]), op=ALU.mult),
                       reads=[psT, smallT], writes=[xdtT[i]])
                yield
                ps, psT = banks.get()
                psb2 = ps[:].bitcast(BF16)
                for q in range(4):
                    OP("pe", lambda e, q=q, psb2=psb2, xcc=xcc: e.transpose(out=psb2[:, q * 128:(q + 1) * 128], in_=xcc[:, 8 + q, :], identity=C["ident_bf"]),
                       reads=[xct[2], cT], writes=[psT], signal=(q == 3))
                OP("act", lambda e, psb2=psb2: e.activation(out=btok[:], in_=psb2[:, 0:512], func=AF.Copy), reads=[psT], writes=[btokT])
                yield
                ps, psT = banks.get()
                for g in range(4):
                    OP("pe", lambda e, g=g, ps=ps, xcc=xcc: e.matmul(ps[:, g * 128:(g + 1) * 128], lhsT=xcc[:, 8 + g, :], rhs=xcc[:, 12 + g, :], start=True, stop=True),
                       reads=[xct[2], xct[3]], writes=[psT], signal=(g == 3))
                for i, tri in enumerate((C["triF_bf"], C["triB_bf"])):
                    OP("dve", lambda e, i=i, tri=tri, ps=ps: e.tensor_tensor(out=cbm[i][:].rearrange("p (g t) -> p g t", g=4), in0=ps[:].rearrange("p (g t) -> p g t", g=4),
                                                                           in1=tri.unsqueeze(1).to_broadcast([128, 4, 128]), op=ALU.mult),
                       reads=[psT, cT], writes=[cbmT[i]])
                yield
                if c > 0:
                    for hf in range(2):
                        ps, psT = banks.get()
                        for j in range(2):
                            g = 2 * hf + j
                            OP("pe", lambda e, g=g, j=j, ps=ps, xcc=xcc: e.matmul(ps[:, j * 256:(j + 1) * 256], lhsT=xcc[:, 12 + g, :], rhs=Sbf[:, g * 256:(g + 1) * 256], start=True, stop=True),
                               reads=[xct[3], SbfT], writes=[psT], signal=(j == 1))
                        OP("dve", lambda e, hf=hf, ps=ps: e.tensor_tensor(out=yoff[:, 512 * hf:512 * hf + 512].rearrange("p (h d) -> p h d", h=8),
                                                                      in0=ps[:].rearrange("p (h d) -> p h d", h=8),
                                                                      in1=ecs[:, 8 * hf:8 * hf + 8].unsqueeze(2).to_broadcast([128, 8, 64]), op=ALU.mult),
                           reads=[psT, smallT], writes=[yoffT])
                        yield
                yield "P"
                segps = {}

                def seg_issue(s_):
                    g, di = s_ // 2, s_ % 2
                    Amat, tri, off = dirs[di]
                    gb = g % 2
                    ps, psT = segbanks[s_ % 3]
                    OP("pe", lambda e, ps=ps, Amat=Amat, gb=gb, di=di: e.matmul(ps[:], lhsT=Amat, rhs=dtri2[gb][:, di, :, :].rearrange("p r t -> p (r t)"), start=True, stop=True),
                       reads=[cT, dtri2T[gb]], writes=[psT])
                    segps[s_] = (ps, psT)

                def em_issue(s_):
                    g, di = s_ // 2, s_ % 2
                    ps, psT = segps[s_]
                    jt = s_ % 2
                    OP("act", lambda e, ps=ps, jt=jt: e.activation(out=Eb[jt][:], in_=ps[:], func=AF.Exp), reads=[psT], writes=[EbT[jt]])
                    mi = s_ % 4
                    OP("pool", lambda e, mi=mi, jt=jt, di=di, g=g: e.tensor_tensor(out=Mb[mi][:].rearrange("p (r t) -> p r t", r=4), in0=Eb[jt][:].rearrange("p (r t) -> p r t", r=4),
                                                                               in1=cbm[di][:, g * 128:(g + 1) * 128].unsqueeze(1).to_broadcast([128, 4, 128]), op=ALU.mult),
                       reads=[EbT[jt], cbmT[di]], writes=[MbT[mi]])

                def y_issue(g):
                    Ms = [(2 * g) % 4, (2 * g + 1) % 4]
                    for r in range(4):
                        h = 4 * g + r
                        q = h // 2
                        pr = (h % 2) * 64
                        psyt, psyT = psy[q // 4]
                        outap = psyt[pr:pr + 64, (q % 4) * 128:(q % 4) * 128 + 128]
                        OP("pe", lambda e, outap=outap, h=h, r=r, m=Ms[0]: e.matmul(outap, lhsT=xdt[0][:, h * 64:(h + 1) * 64], rhs=Mb[m][:, r * 128:(r + 1) * 128], start=True, stop=False),
                           reads=[xdtT[0], MbT[Ms[0]]], writes=[psyT], signal=False)
                        OP("pe", lambda e, outap=outap, h=h, r=r, m=Ms[1]: e.matmul(outap, lhsT=xdt[1][:, h * 64:(h + 1) * 64], rhs=Mb[m][:, r * 128:(r + 1) * 128], start=False, stop=(c == 0)),
                           reads=[xdtT[1], MbT[Ms[1]]], writes=[psyT], signal=(c == 0 and (r == 3)))
                        if c > 0:
                            OP("pe", lambda e, outap=outap, h=h: e.matmul(outap, lhsT=yoff[:, h * 64:(h + 1) * 64], rhs=C["ident_bf"], start=False, stop=True),
                               reads=[yoffT, cT], writes=[psyT], signal=(r == 3))

                seg_issue(0)
                seg_issue(1)
                yield
                for s_ in range(8):
                    if s_ == 1:
                        dtri_issue(2)
                    if s_ == 3:
                        dtri_issue(3)
                    if s_ + 2 < 8:
                        seg_issue(s_ + 2)
                    em_issue(s_)
                    if s_ >= 3 and s_ % 2 == 1:
                        y_issue((s_ - 3) // 2)
                    yield
                y_issue(3)
                yield "P"
                for q in range(8):
                    psyt, psyT = psy[q // 4]
                    OP("dve", lambda e, q=q, psyt=psyt, xcc=xcc: e.scalar_tensor_tensor(out=ypart[:, q, :], in0=xcc[:, q, :], scalar=dsk[:, q:q + 1],
                                                                                    in1=psyt[:, (q % 4) * 128:(q % 4) * 128 + 128], op0=ALU.mult, op1=ALU.add),
                       reads=[xct[q // 4], prT, psyT], writes=[ypT])
                OP("sp", lambda e, c=c: e.dma_start(out=YP[c], in_=ypart[:].rearrange("p q t -> p (q t)")), reads=[ypT], writes=[T()], dma=True)
                yield
                for di in range(2):
                    pss = [banks.get() for _ in range(2)]
                    for g in range(4):
                        pst, pstT = pss[g // 2]
                        OP("pe", lambda e, g=g, pst=pst, di=di: e.matmul(pst[:, (g % 2) * 256:(g % 2) * 256 + 256], lhsT=btok[:, g * 128:(g + 1) * 128], rhs=xdt[2 + di][:, g * 256:(g + 1) * 256],
                                                                      start=True, stop=True), reads=[btokT, xdtT[2 + di]], writes=[pstT], signal=(g % 2 == 1))
                    if di == 0:
                        OP("dve", lambda e: e.tensor_tensor(out=Sf[:].rearrange("p (h d) -> p h d", h=16), in0=Sf[:].rearrange("p (h d) -> p h d", h=16),
                                                            in1=etot[:, 0:16].unsqueeze(2).to_broadcast([128, 16, 64]), op=ALU.mult), reads=[SfT, smallT], writes=[SfT])
                        for hf in range(2):
                            OP("act", lambda e, hf=hf, pst=pss[hf][0]: e.activation(out=stf[:, 512 * hf:512 * hf + 512], in_=pst[:], func=AF.Copy), reads=[pss[hf][1]], writes=[stfT])
                        OP("dve", lambda e: e.tensor_tensor(out=Sf[:], in0=Sf[:], in1=stf[:], op=ALU.add), reads=[SfT, stfT], writes=[SfT])
                        OP("act", lambda e: e.activation(out=Sbf[:], in_=Sf[:], func=AF.Copy), reads=[SfT], writes=[SbfT])
                    else:
                        for hf in range(2):
                            OP("act", lambda e, hf=hf, pst=pss[hf][0]: e.activation(out=stb[:, 512 * hf:512 * hf + 512], in_=pst[:], func=AF.Copy), reads=[pss[hf][1]], writes=[stbT])
                    yield
                OP("sp", lambda e, c=c, xcc=xcc: e.dma_start(out=CTs[c], in_=xcc[:, 12:16, :].rearrange("p q t -> p (q t)")), reads=[xct[3]], writes=[T()], dma=True)
                OP("sp", lambda e, c=c: e.dma_start(out=STB[c], in_=stb[:]), reads=[stbT], writes=[T()], dma=True)
            def interleave(*gens):
                gens = [g for g in gens if g is not None]
                while gens:
                    for g in list(gens):
                        try:
                            next(g)
                        except StopIteration:
                            gens.remove(g)

            def run(gens, lead=0):
                live = list(gens)
                for _ in range(lead):
                    try:
                        if next(live[0]) == "P":
                            live.pop(0)
                            break
                    except StopIteration:
                        live.pop(0)
                        break
                while live:
                    for g in list(live):
                        try:
                            if next(g) == "P":
                                live.remove(g)
                        except StopIteration:
                            live.remove(g)

            norm_a(0)
            norm_b(0)
            bprev = None
            for c in range(NCH):
                f = front(c)
                banks.rot = list(rot_base)
                run([f] + ([bprev] if bprev is not None else []))
                banks.rot = list(rot_base) + seg_ids
                run([f] + ([bprev] if bprev is not None else []))
                bcur = back(c)
                run([f, bcur], lead=2)
                if c >= 1:
                    run_hooks(1)
                bprev = bcur
            run([bprev]); run([bprev])
            banks.release()
            barrier(P)

    def sweepB():
        with ExitStack() as es:
            sb = lambda name, shape, dt: es.enter_context(nc.sbuf_tensor(f"{name}_{_uid()}", shape, dt))
            OP = P.op
            wout = sb("b_wout", [128, 16, D], BF16); woutT = T()
            for k4 in range(0, 16, 4):
                OP("sp", lambda e, k4=k4: e.dma_start(out=wout[:, k4:k4 + 4, :], in_=prm["wout_b"][k4 * 128:(k4 + 4) * 128, :].rearrange("(k p) n -> p k n", p=128)),
                   reads=prm["wT"], writes=[woutT], dma=True)
            prT = T()
            ssdg = sb("b_ssdg", [128, 8], F32); gpost = sb("b_gpost", [128, 8], F32)
            OP("sp", lambda e: e.dma_start(out=ssdg[:], in_=prm["ssdg"]), writes=[prT], dma=True)
            OP("sp", lambda e: e.dma_start(out=gpost[:], in_=prm["gpost"]), writes=[prT], dma=True)
            NB = 4
            yp = [sb(f"b_yp{i}", [128, 8, 128], F32) for i in range(NB)]
            zs = [sb(f"b_zs{i}", [128, 8, 128], BF16) for i in range(NB)]
            yg4 = [sb(f"b_yg4{i}", [128, 8, 512], BF16) for i in range(3)]; yg4T = [[T() for _ in range(4)] for _ in range(3)]
            ct = [sb(f"b_ct{i}", [128, 4, 128], BF16) for i in range(NB)]
            sm = [sb(f"b_sm{i}", [128, 32], F32) for i in range(NB)]
            stb = [sb(f"b_stb{i}", [128, 1024], F32) for i in range(NB)]
            xin4 = sb("b_xin4", [128, 8, 512], F32); xin4T = T()
            ldT = [[T() for _ in range(6)] for _ in range(NB)]
            Sb = sb("b_S", [128, 1024], F32); SbT = T()
            Sbb_ = [sb(f"b_Sbf{i}", [128, 1024], BF16) for i in range(2)]; SbbT_ = [T() for _ in range(2)]
            yoff_ = [sb(f"b_yoff{i}", [128, 1024], BF16) for i in range(2)]; yoffT_ = [T() for _ in range(2)]
            y_ = [sb(f"b_y{i}", [128, 8, 128], F32) for i in range(2)]; yT_ = [T() for _ in range(2)]
            sq_ = [sb(f"b_sq{i}", [128, 8, 128], BF16) for i in range(2)]; sqT_ = [T() for _ in range(2)]
            rstd_ = [sb(f"b_rstd{i}", [128, 128], F32) for i in range(2)]; rstdT_ = [T() for _ in range(2)]
            yssd4 = [sb(f"b_yssd4{i}", [128, 8, 512], BF16) for i in range(2)]; yssd4T = [[T() for _ in range(4)] for _ in range(2)]
            sq2 = sb("b_sq2", [128, 8, 512], BF16); sq2T = [T() for _ in range(8)]
            fsb = sb("b_fsb", [128, 8, 512], F32); fsbT = [T() for _ in range(8)]
            rstd2 = sb("b_rstd2", [128, 512], F32); rstd2T = T()
            tmp = [sb(f"b_tmp{i}", [128, 512], F32) for i in range(2)]; tmpT = [T() for _ in range(2)]
            OP("dve", lambda e: e.memset(Sb[:], 0.0), writes=[SbT])

            def loads(c):
                b = c % NB
                OP("sp", lambda e: e.dma_start(out=yp[b][:].rearrange("p q t -> p (q t)"), in_=YP[c]), writes=[ldT[b][0]], dma=True)
                OP("sp", lambda e: e.dma_start(out=zs[b][:].rearrange("p q t -> p (q t)"), in_=ZS[c]), writes=[ldT[b][1]], dma=True)
                tl, j = c // 4, c % 4
                OP("sp", lambda e: e.dma_start(out=yg4[tl % 3][:, :, j * 128:(j + 1) * 128], in_=YG[c].rearrange("p (q t) -> p q t", q=8)), writes=[yg4T[tl % 3][j]], dma=True)
                OP("sp", lambda e: e.dma_start(out=ct[b][:].rearrange("p q t -> p (q t)"), in_=CTs[c]), writes=[ldT[b][3]], dma=True)
                OP("sp", lambda e: e.dma_start(out=sm[b][:], in_=SM[c]), writes=[ldT[b][4]], dma=True)
                OP("sp", lambda e: e.dma_start(out=stb[b][:], in_=STB[c]), writes=[ldT[b][5]], dma=True)

            order = list(range(NCH_RUN - 1, -1, -1))
            loads(order[0])
            def b1a(ci, c):
                b = c % NB
                if ci + 1 < len(order):
                    loads(order[ci + 1])
                lt = ldT[b]
                yoff = yoff_[ci % 2]; yoffT = yoffT_[ci % 2]
                if ci > 0:
                    Sbb = Sbb_[ci % 2]; SbbT = SbbT_[ci % 2]
                    for hf in range(2):
                        ps, psT = banks.get()
                        for j in range(2):
                            g = 2 * hf + j
                            OP("pe", lambda e, g=g, j=j, ps=ps: e.matmul(ps[:, j * 256:(j + 1) * 256], lhsT=ct[b][:, g, :], rhs=Sbb[:, g * 256:(g + 1) * 256], start=True, stop=True),
                               reads=lt + [SbbT], writes=[psT], signal=(j == 1))
                        OP("dve", lambda e, hf=hf, ps=ps: e.tensor_tensor(out=yoff[:, 512 * hf:512 * hf + 512].rearrange("p (h d) -> p h d", h=8),
                                                                      in0=ps[:].rearrange("p (h d) -> p h d", h=8),
                                                                      in1=sm[b][:, 8 * hf:8 * hf + 8].unsqueeze(2).to_broadcast([128, 8, 64]), op=ALU.mult),
                           reads=[psT] + lt, writes=[yoffT])
                        yield
                if ci + 1 < len(order):
                    nS = Sbb_[(ci + 1) % 2]; nST = SbbT_[(ci + 1) % 2]
                    OP("dve", lambda e: e.tensor_tensor(out=Sb[:].rearrange("p (h d) -> p h d", h=16), in0=Sb[:].rearrange("p (h d) -> p h d", h=16),
                                                        in1=sm[b][:, 16:32].unsqueeze(2).to_broadcast([128, 16, 64]), op=ALU.mult), reads=[SbT] + lt, writes=[SbT])
                    OP("dve", lambda e: e.tensor_tensor(out=Sb[:], in0=Sb[:], in1=stb[b][:], op=ALU.add), reads=[SbT] + lt, writes=[SbT])
                    OP("act", lambda e: e.activation(out=nS[:], in_=Sb[:], func=AF.Copy), reads=[SbT], writes=[nST])
                    yield

            def b1b(ci, c):
                b = c % NB
                lt = ldT[b]
                yoff = yoff_[ci % 2]; yoffT = yoffT_[ci % 2]
                y = y_[ci % 2]; yT = yT_[ci % 2]; sq = sq_[ci % 2]; sqT = sqT_[ci % 2]
                if ci > 0:
                    for hf in range(2):
                        ps, psT = banks.get()
                        for j in range(4):
                            q = 4 * hf + j
                            OP("pe", lambda e, q=q, j=j, ps=ps: e.matmul(ps[:, j * 128:(j + 1) * 128], lhsT=yoff[:, q * 128:(q + 1) * 128], rhs=C["ident_bf"], start=True, stop=True),
                               reads=[yoffT, cT], writes=[psT], signal=(j == 3))
                        OP("dve", lambda e, hf=hf, ps=ps: e.tensor_tensor(out=y[:, 4 * hf:4 * hf + 4, :], in0=ps[:].rearrange("p (q t) -> p q t", q=4), in1=yp[b][:, 4 * hf:4 * hf + 4, :], op=ALU.add),
                           reads=[psT] + lt, writes=[yT])
                        yield
                    OP("dve", lambda e: e.tensor_tensor(out=y[:], in0=y[:], in1=zs[b][:], op=ALU.mult), reads=[yT] + lt, writes=[yT])
                else:
                    OP("dve", lambda e: e.tensor_tensor(out=y[:], in0=yp[b][:], in1=zs[b][:], op=ALU.mult), reads=lt, writes=[yT])
                OP("act", lambda e: e.activation(out=sq[:], in_=y[:], func=AF.Square), reads=[yT], writes=[sqT])
                yield

            def b15(ci, c):
                y = y_[ci % 2]; yT = yT_[ci % 2]; sq = sq_[ci % 2]; sqT = sqT_[ci % 2]
                rstd = rstd_[ci % 2]; rstdT = rstdT_[ci % 2]
                tl, j = c // 4, c % 4
                yssd = yssd4[tl % 2]; yssdT = yssd4T[tl % 2][j]
                ps, psT = banks.get()
                for k in range(NK):
                    OP("pe", lambda e, k=k, ps=ps: e.matmul(ps[:, 0:128], lhsT=C["ones_bf"], rhs=sq[:, k, :], start=(k == 0), stop=(k == NK - 1)),
                       reads=[sqT, cT], writes=[psT], signal=(k == NK - 1))
                rstd_from_ss(P, ps[:, 0:128], psT, rstd[:], rstdT, D)
                OP("pool", lambda e: e.tensor_tensor(out=y[:], in0=y[:], in1=rstd[:].unsqueeze(1).to_broadcast([128, 8, 128]), op=ALU.mult), reads=[yT, rstdT], writes=[yT])
                OP("pool", lambda e: e.tensor_tensor(out=yssd[:, :, j * 128:(j + 1) * 128], in0=y[:], in1=ssdg[:].unsqueeze(2).to_broadcast([128, 8, 128]), op=ALU.mult),
                   reads=[yT, prT], writes=[yssdT])
                yield

            def b2(tl):
                yssd = yssd4[tl % 2]; yssdTs = yssd4T[tl % 2]
                ygt = yg4[tl % 3]; ygTs = yg4T[tl % 3]
                OP("sp", lambda e: e.dma_start(out=xin4[:], in_=srcv[:, :, tl * 512:(tl + 1) * 512]), writes=[xin4T], dma=True)
                for d in range(8):
                    pst, pstT = banks.get()
                    for k in range(16):
                        rhs = yssd[:, k, :] if k < 8 else ygt[:, k - 8, :]
                        OP("pe", lambda e, d=d, k=k, rhs=rhs, pst=pst: e.matmul(pst[:], lhsT=wout[:, k, d * 128:(d + 1) * 128], rhs=rhs, start=(k == 0), stop=(k == 15)),
                           reads=[woutT] + yssdTs + ygTs, writes=[pstT], signal=(k == 15))
                    OP("act", lambda e, d=d, pst=pst: e.activation(out=fsb[:, d, :], in_=pst[:], func=AF.Copy), reads=[pstT], writes=[fsbT[d]])
                    OP("act", lambda e, d=d, pst=pst: e.activation(out=sq2[:, d, :], in_=pst[:], func=AF.Square), reads=[pstT], writes=[sq2T[d]])
                    yield
                ps, psT = banks.get()
                for k in range(NK):
                    OP("pe", lambda e, k=k, ps=ps: e.matmul(ps[:], lhsT=C["ones_bf"], rhs=sq2[:, k, :], start=(k == 0), stop=(k == NK - 1)),
                       reads=[sq2T[k], cT], writes=[psT], signal=(k == NK - 1))
                rstd_from_ss(P, ps[:], psT, rstd2[:], rstd2T, D)
                yield
                for d in range(8):
                    jb = d % 2
                    OP("dve", lambda e, d=d, jb=jb: e.scalar_tensor_tensor(out=tmp[jb][:], in0=fsb[:, d, :], scalar=gpost[:, d:d + 1], in1=rstd2[:], op0=ALU.mult, op1=ALU.mult),
                       reads=[fsbT[d], rstd2T, prT], writes=[tmpT[jb]])
                    OP("pool", lambda e, d=d, jb=jb: e.tensor_tensor(out=fsb[:, d, :], in0=tmp[jb][:], in1=xin4[:, d, :], op=ALU.add), reads=[tmpT[jb], xin4T], writes=[fsbT[d]])
                    if d % 2 == 1:
                        yield
                oT = T()
                outs.append(oT)
                OP("act", lambda e: e.dma_start(out=dstv[:, :, tl * 512:(tl + 1) * 512], in_=fsb[:]), reads=fsbT, writes=[oT], dma=True)
                yield

            def interleave(*gens):
                gens = [g for g in gens if g is not None]
                while gens:
                    for g in list(gens):
                        try:
                            next(g)
                        except StopIteration:
                            gens.remove(g)

            n_ = len(order)
            b2q = []

            def adv_b2(k):
                for _ in range(k):
                    if not b2q:
                        return
                    try:
                        next(b2q[0])
                    except StopIteration:
                        b2q.pop(0)

            for t_ in range(n_ + 2):
                gens = [g for g in (b15(t_ - 2, order[t_ - 2]) if 0 <= t_ - 2 < n_ else None,
                                    b1b(t_ - 1, order[t_ - 1]) if 0 <= t_ - 1 < n_ else None,
                                    b1a(t_, order[t_]) if t_ < n_ else None) if g is not None]
                while gens:
                    adv_b2(1)
                    for g in list(gens):
                        try:
                            next(g)
                        except StopIteration:
                            gens.remove(g)
                adv_b2(2)
                if 0 <= t_ - 2 < n_ and order[t_ - 2] % 4 == 0:
                    b2q.append(b2(order[t_ - 2] // 4))
            while b2q:
                adv_b2(1)
            barrier(P)

    sweepA()
    sweepB()
    return outs


def make_consts():
    k = np.arange(128)[:, None]; i = np.arange(128)[None, :]
    c = np.stack([np.eye(128), (k <= i), (k >= i), (k > i), (k < i), np.ones((128, 128))], axis=1).astype(np.float32)
    return np.ascontiguousarray(c)

def pp(v):
    return np.ascontiguousarray(v.reshape(8, 128).T)

def prep_mixer(I, l):
    return {
        "win": np.ascontiguousarray(I["w_in"][l]), "wout": np.ascontiguousarray(I["w_out"][l]),
        "convw": np.ascontiguousarray(I["conv_w"][l].T.reshape(16, 128, 5).transpose(1, 0, 2)),
        "convb": np.ascontiguousarray(I["conv_b"][l].reshape(16, 128).T),
        "dtbias": np.ascontiguousarray(np.tile(np.concatenate([I["dt_bias_f"][l], I["dt_bias_b"][l]])[None, :], (128, 1))),
        "alog": np.ascontiguousarray(np.tile(np.concatenate([I["a_log_f"][l], I["a_log_b"][l]])[None, :], (128, 1))),
        "dsk": pp(np.repeat(I["d_skip"][l], 64)), "ssdg": pp(I["ssd_norm_g"][l]),
        "gpre": pp(I["mix_pre_g"][l]), "gpost": pp(I["mix_post_g"][l]),
        "lng": np.ascontiguousarray(np.tile(I["gmlp_ln_g"][l][None, :], (128, 1))),
        "lnb": np.ascontiguousarray(np.tile(I["gmlp_ln_b"][l][None, :], (128, 1))),
        "swT": np.ascontiguousarray(I["spatial_w"][l].transpose(2, 0, 1)),
        "sbrow": np.ascontiguousarray(I["spatial_b"][l].reshape(1, 1024)),
    }
MIX_SHAPES = {"win": [1024, 5152], "wout": [2048, 1024], "convw": [128, 16, 5], "convb": [128, 16], "dtbias": [128, 32], "alog": [128, 32],
              "dsk": [128, 8], "ssdg": [128, 8], "gpre": [128, 8], "gpost": [128, 8], "lng": [128, 1024], "lnb": [128, 1024],
              "swT": [128, 8, 128], "sbrow": [1, 1024]}

def declare_mixer(nc, P, sfx):
    d = {k: nc.dram_tensor(f"m{sfx}_{k}", shp, F32, kind="ExternalInput").ap() for k, shp in MIX_SHAPES.items()}
    win_b = nc.dram_tensor(f"m{sfx}_win_b", [1024, 5152], BF16).ap()
    wout_b = nc.dram_tensor(f"m{sfx}_wout_b", [2048, 1024], BF16).ap()
    wT = T()
    for r in range(0, 1024, 256):
        P.op("pool", lambda e, r=r: e.dma_start(out=win_b[r:r + 256].rearrange("r (a b) -> (r a) b", a=4), in_=d["win"][r:r + 256].rearrange("r (a b) -> (r a) b", a=4)),
             writes=[wT], dma=True)
    for r in range(0, 2048, 1024):
        P.op("pool", lambda e, r=r: e.dma_start(out=wout_b[r:r + 1024], in_=d["wout"][r:r + 1024]), writes=[wT], dma=True)
    prm = dict(d); prm["win_b"] = win_b; prm["wout_b"] = wout_b; prm["wT"] = [wT]
    return prm

def declare_scratch(nc):
    return {"YP": nc.dram_tensor("s_YP", [NCH, 128, 1024], F32).ap(), "ZS": nc.dram_tensor("s_ZS", [NCH, 128, 1024], BF16).ap(),
            "YG": nc.dram_tensor("s_YG", [NCH, 128, 1024], BF16).ap(), "CT": nc.dram_tensor("s_CT", [NCH, 128, 512], BF16).ap(),
            "SM": nc.dram_tensor("s_SM", [NCH, 128, 32], F32).ap(), "STB": nc.dram_tensor("s_STB", [NCH, 128, 1024], F32).ap()}


from concourse.bass_utils import run_bass_kernel_spmd

DEPTH = 4
N_CORES = 8
FFN_KEYS = ("wg", "wu", "wd")


def lay_gu(w):
    return np.ascontiguousarray(w.reshape(8, 128, NM, 128).transpose(2, 1, 0, 3).reshape(NM, 128, 1024))


def declare_ffn(nc, tag):
    d = {k: nc.dram_tensor(f"{tag}_{k}", [NM, 128, 1024], F32, kind="ExternalInput").ap() for k in FFN_KEYS}
    d["gpre"] = nc.dram_tensor(f"{tag}_gpre", [128, 8], F32, kind="ExternalInput").ap()
    d["gpost"] = nc.dram_tensor(f"{tag}_gpost", [128, 8], F32, kind="ExternalInput").ap()
    for k in FFN_KEYS:
        d[k + "_b"] = nc.dram_tensor(f"{tag}_{k}_b", [NM, 128, 1024], BF16).ap()
    d["wT"] = [T() for _ in FFN_KEYS]
    return d


def cast_ffn(P, d):
    pieces = []
    for j, k in enumerate(FFN_KEYS):
        for i in range(0, NM, 6):
            n = min(6, NM - i)
            pieces.append(lambda i=i, k=k, j=j, n=n: P.op("pool", lambda e: e.dma_start(out=d[k + "_b"][i:i + n], in_=d[k][i:i + n]),
                                                       writes=[d["wT"][j]], dma=True))
    return pieces


def declare_mixer2(nc, sfx):
    d = {k: nc.dram_tensor(f"m{sfx}_{k}", shp, F32, kind="ExternalInput").ap() for k, shp in MIX_SHAPES.items()}
    d["win_b"] = nc.dram_tensor(f"m{sfx}_win_b", [1024, 5152], BF16).ap()
    d["wout_b"] = nc.dram_tensor(f"m{sfx}_wout_b", [2048, 1024], BF16).ap()
    d["wT"] = [T()]
    return d


def cast_mixer(P, d):
    wT = d["wT"][0]
    pieces = []
    for r in range(0, 1024, 128):
        pieces.append(lambda r=r: P.op("pool", lambda e: e.dma_start(out=d["win_b"][r:r + 128].rearrange("r (a b) -> (r a) b", a=4),
                                                                    in_=d["win"][r:r + 128].rearrange("r (a b) -> (r a) b", a=4)), writes=[wT], dma=True))
    for r in range(0, 2048, 512):
        pieces.append(lambda r=r: P.op("pool", lambda e: e.dma_start(out=d["wout_b"][r:r + 512], in_=d["wout"][r:r + 512]), writes=[wT], dma=True))
    return pieces


HOOKS = []


def run_hooks(n):
    for _ in range(min(n, len(HOOKS))):
        HOOKS.pop(0)()


def build_program():
    nc = bass.Bass("TRN2", target_bir_lowering=False)
    x = nc.dram_tensor("x", [D, L], F32, kind="ExternalInput").ap()
    consts = nc.dram_tensor("consts", [128, 6, 128], F32, kind="ExternalInput").ap()
    y = nc.dram_tensor("y", [D, L], F32, kind="ExternalOutput").ap()
    X = nc.dram_tensor("Xres", [D, L], F32).ap()
    P = Prog(nc, nsem_compute=8, nsem_dma=12)
    with ExitStack() as es:
        banks = Banks(nc, es)
        C = load_consts(P, nc, es, consts)
        scr = declare_scratch(nc)
        stages = []
        for l in range(DEPTH):
            stages.append(("ffn", declare_ffn(nc, f"l{l}f1")))
            stages.append(("mix", declare_mixer2(nc, f"{l}")))
            stages.append(("ffn", declare_ffn(nc, f"l{l}f2")))

        def cast(i):
            kind, d = stages[i]
            return (cast_ffn if kind == "ffn" else cast_mixer)(P, d)

        for pc in cast(0):
            pc()
        outs = []
        for i, (kind, d) in enumerate(stages):
            HOOKS.clear()
            if i + 1 < len(stages):
                HOOKS.extend(cast(i + 1))
            src = x if i == 0 else X
            dst = y if i == len(stages) - 1 else X
            if kind == "ffn":
                outs = ffn_stage(P, nc, banks, src, dst, d["wg_b"], d["wu_b"], d["wd_b"], d["gpre"], d["gpost"], d["wT"], C["ones_bf"], C["T"])
            else:
                outs = mixer_stage(P, nc, banks, src, dst, d, scr, C)
            run_hooks(len(HOOKS))
        P.build(final_waits=outs)
    return nc


def kernel(**I):
    I = {k: np.asarray(v) for k, v in I.items()}
    shared = {"consts": make_consts()}
    for l in range(DEPTH):
        for f in ("ff1", "ff2"):
            tag = f"l{l}f{f[2]}"
            shared[f"{tag}_wg"] = lay_gu(I[f + "_w_gate"][l])
            shared[f"{tag}_wu"] = lay_gu(I[f + "_w_up"][l])
            shared[f"{tag}_wd"] = np.ascontiguousarray(I[f + "_w_down"][l].reshape(NM, 128, 1024))
            shared[f"{tag}_gpre"] = pp(I[f + "_pre_g"][l])
            shared[f"{tag}_gpost"] = pp(I[f + "_post_g"][l])
        for k, v in prep_mixer(I, l).items():
            shared[f"m{l}_{k}"] = v
    x = I["x"]
    in_maps = []
    for b in range(N_CORES):
        m = dict(shared)
        m["x"] = np.ascontiguousarray(x[b].T)
        in_maps.append(m)
    nc = build_program()
    res = run_bass_kernel_spmd(nc, in_maps, core_ids=list(range(N_CORES)))
    out = np.stack([np.ascontiguousarray(r["y"].T) for r in res.results], axis=0)
    return out.astype(np.float32)
```

```python
import numpy as np
import concourse.bass as bass
import concourse.mybir as mybir
from contextlib import ExitStack

F32 = mybir.dt.float32
BF16 = mybir.dt.bfloat16
AF = mybir.ActivationFunctionType
ALU = mybir.AluOpType
AX = mybir.AxisListType

ENGS = ("pe", "act", "dve", "pool", "sp")
_UIDC = [0]


def _uid():
    _UIDC[0] += 1
    return _UIDC[0]
DMA_INC = 16


class T:
    __slots__ = ("name", "w", "r")

    def __init__(self, name=""):
        self.name = name
        self.w = None
        self.r = {}


class EngQ:
    def __init__(self, name, nsem, is_dma=False):
        self.name = name
        self.nsem = nsem
        self.is_dma = is_dma
        self.sems = []
        self.count = 0
        self.stream = []
        self.seen = {}


class Prog:
    def __init__(self, nc, nsem_compute=4, nsem_dma=12):
        self.nc = nc
        self.q = {e: EngQ(e, nsem_compute) for e in ENGS}
        for e in ("sp", "act", "pool"):
            self.q["d" + e] = EngQ("d" + e, 6 if e == "pool" else nsem_dma, is_dma=True)
        self.n_ops = 0

    def _semval(self, q, iid):
        k = q.nsem
        idx = (iid - 1) % k
        val = (iid - 1) // k + 1
        if q.is_dma:
            val *= DMA_INC
        return idx, val

    def op(self, eng, fn, reads=(), writes=(), signal=True, dma=False):
        q = self.q[eng]
        need = {}

        def add(dep):
            if dep is None:
                return
            e, i = dep
            qq = self.q[e]
            key = (e, (i - 1) % qq.nsem) if qq.is_dma else e
            if need.get(key, (None, 0))[1] < i:
                need[key] = (e, i)

        for t in reads:
            add(t.w)
        for t in writes:
            add(t.w)
            for d in t.r.values():
                add(d)
        qsig = self.q["d" + eng] if dma else q
        if dma:
            if not signal:
                raise RuntimeError("dma must signal")
            nxt = qsig.count + 1
            if nxt > qsig.nsem:
                add((qsig.name, nxt - qsig.nsem))
        waits = []
        for key, (e, i) in need.items():
            if e == "pe" and eng == "pe" and not dma:
                continue
            if q.seen.get(key, 0) >= i:
                continue
            if i > self.q[e].count:
                raise RuntimeError(f"dependency on future/unsignalled instr {(e, i)} (count {self.q[e].count})")
            q.seen[key] = i
            idx, val = self._semval(self.q[e], i)
            waits.append((e, idx, val))
        if signal:
            qsig.count += 1
            iid = qsig.count
            sidx, _ = self._semval(qsig, iid)
            sig = (qsig.name, sidx, DMA_INC if dma else 1)
            myid = (qsig.name, iid)
        else:
            sig = None
            myid = (qsig.name, qsig.count + 1)
        q.stream.append((waits, fn, sig))
        self.n_ops += 1
        qn, qi = myid
        rkey = (qn, (qi - 1) % qsig.nsem) if qsig.is_dma else qn
        for t in reads:
            old = t.r.get(rkey)
            if old is None or old[1] < qi:
                t.r[rkey] = myid
        for t in writes:
            t.w = myid
            t.r = {}
        return myid

    def build(self, final_waits=()):
        nc = self.nc
        if final_waits:
            self.op("sp", None, reads=list(final_waits), signal=False)
        with ExitStack() as es:
            for name, q in self.q.items():
                if q.count == 0:
                    continue
                q.sems = [es.enter_context(nc.semaphore(f"s_{name}_{i}")) for i in range(q.nsem)]
            block = es.enter_context(nc.Block())
            handles = {"pe": block.tensor, "act": block.scalar, "dve": block.vector,
                       "pool": block.gpsimd, "sp": block.sync}
            for name in ENGS:
                q = self.q[name]
                if not q.stream:
                    continue

                def body(e, q=q):
                    for waits, fn, sig in q.stream:
                        for (en, idx, val) in waits:
                            e.wait_ge(self.q[en].sems[idx], val)
                        if fn is None:
                            continue
                        ins = fn(e)
                        if sig is not None:
                            ins.then_inc(self.q[sig[0]].sems[sig[1]], sig[2])
                handles[name](body)


def barrier(P):
    targets = []
    for name, qq in P.q.items():
        if qq.count == 0:
            continue
        if qq.is_dma:
            for i in range(max(1, qq.count - qq.nsem + 1), qq.count + 1):
                targets.append((name, i))
        else:
            targets.append((name, qq.count))
    for e in ENGS:
        q = P.q[e]
        waits = []
        for (n, i) in targets:
            qq = P.q[n]
            if n == e and e == "pe":
                continue
            key = (n, (i - 1) % qq.nsem) if qq.is_dma else n
            if q.seen.get(key, 0) >= i:
                continue
            q.seen[key] = i
            idx, val = P._semval(qq, i)
            waits.append((n, idx, val))
        if waits:
            q.stream.append((waits, None, None))


class Banks:
    def __init__(self, nc, es):
        self.t = [es.enter_context(nc.psum_tensor(f"psb{i}", [128, 512], F32)) for i in range(8)]
        self.T = [T(f"psb{i}") for i in range(8)]
        self.rot = list(range(8))
        self.i = 0

    def get(self):
        i = self.rot[self.i % len(self.rot)]
        self.i += 1
        return self.t[i], self.T[i]

    def reserve(self, n):
        r = self.rot[-n:]
        self.rot = self.rot[:-n]
        return [(self.t[i], self.T[i]) for i in r]

    def release(self):
        self.rot = list(range(8))


D = 1024
L = 4096
DFF = 2816
NM = DFF // 128
NK = D // 128
EPS = 1e-6
TT = 512
NT = L // TT


def rstd_from_ss(P, ss_ps, ssT, rstd, rT, n):
    P.op("act", lambda e: e.activation(out=rstd, in_=ss_ps, func=AF.Ln, scale=1.0 / n, bias=EPS), reads=[ssT], writes=[rT])
    P.op("act", lambda e: e.activation(out=rstd, in_=rstd, func=AF.Exp, scale=-0.5), reads=[rT], writes=[rT])


def ffn_stage(P, nc, banks, src, dst, wg_d, wu_d, wd_d, gpre_d, gpost_d, wT, ones_bf, onesT):
    srcv = src.rearrange("(k p) t -> p k t", p=128)
    dstv = dst.rearrange("(k p) t -> p k t", p=128)
    with ExitStack() as es:
        sb = lambda name, shape, dt: es.enter_context(nc.sbuf_tensor(f"{name}_{_uid()}", shape, dt))
        xt = [sb(f"f_xt{i}", [128, NK, TT], F32) for i in range(2)]
        xtT = [T() for _ in range(2)]
        sq = sb("f_sq", [128, NK, TT], BF16); sqT = T()
        h = [sb(f"f_h{i}", [128, NK, TT], BF16) for i in range(2)]
        hT = [T() for _ in range(2)]
        a = sb("f_a", [128, NM, TT], BF16)
        aT = [T() for _ in range(NM)]
        wd = sb("f_wd", [128, NM, D], BF16); wdT = T()
        NSL = 3
        wgs = [sb(f"f_wg{i}", [128, 2, 1024], BF16) for i in range(NSL)]
        wus = [sb(f"f_wu{i}", [128, 2, 1024], BF16) for i in range(NSL)]
        wgT = [T() for _ in range(NSL)]
        wuT = [T() for _ in range(NSL)]
        sg = [sb(f"f_sg{i}", [128, TT], BF16) for i in range(2)]
        sgT = [T() for _ in range(2)]
        fsb = sb("f_fsb", [128, NK, TT], F32); fsbT = [T() for _ in range(NK)]
        sq2 = sq; sq2T = [sqT for _ in range(NK)]
        rstd = [sb(f"f_rstd{i}", [128, TT], F32) for i in range(2)]
        rstdT = [T() for _ in range(2)]
        rstd2 = sb("f_rstd2", [128, TT], F32); rstd2T = T()
        tmp = [sb(f"f_tmp{i}", [128, TT], F32) for i in range(2)]
        tmpT = [T() for _ in range(2)]
        gpre = sb("f_gpre", [128, NK], F32); gpost = sb("f_gpost", [128, NK], F32)
        gT = T()

        P.op("sp", lambda e: e.dma_start(out=gpre[:], in_=gpre_d), writes=[gT], dma=True)
        P.op("sp", lambda e: e.dma_start(out=gpost[:], in_=gpost_d), writes=[gT], dma=True)
        P.op("dve", lambda e: e.tensor_scalar(out=gpost[:], in0=gpost[:], scalar1=0.5, scalar2=None, op0=ALU.mult),
             reads=[gT], writes=[gT])
        def load_wd():
            for q4 in range(0, NM, 6):
                n = min(6, NM - q4)
                P.op("sp", lambda e, q4=q4, n=n: e.dma_start(out=wd[:, q4:q4 + n, :],
                                                           in_=wd_d[q4:q4 + n].rearrange("m p n -> p m n")),
                     reads=wT, writes=[wdT], dma=True)

        def xload(t):
            b = t % 2
            P.op("sp", lambda e: e.dma_start(out=xt[b][:], in_=srcv[:, :, t * TT:(t + 1) * TT]),
                 writes=[xtT[b]], dma=True)

        def prologue(t):
            b = t % 2
            P.op("act", lambda e: e.activation(out=sq[:], in_=xt[b][:], func=AF.Square), reads=[xtT[b]], writes=[sqT])
            ps, psT = banks.get()
            for k in range(NK):
                P.op("pe", lambda e, k=k: e.matmul(ps[:], lhsT=ones_bf, rhs=sq[:, k, :], start=(k == 0), stop=(k == NK - 1)),
                     reads=[sqT, onesT], writes=[psT], signal=(k == NK - 1))
            rstd_from_ss(P, ps[:], psT, rstd[b][:], rstdT[b], D)
            for k in range(NK):
                P.op("dve", lambda e, k=k: e.scalar_tensor_tensor(out=h[b][:, k, :], in0=xt[b][:, k, :], scalar=gpre[:, k:k + 1],
                                                                  in1=rstd[b][:], op0=ALU.mult, op1=ALU.mult),
                     reads=[xtT[b], rstdT[b], gT], writes=[hT[b]])

        slab_ctr = [0]
        outs = []

        def gateup(t):
            b = t % 2
            for mp in range(NM // 2):
                s = slab_ctr[0] % NSL
                slab_ctr[0] += 1
                P.op("sp", lambda e, s=s, mp=mp: e.dma_start(out=wgs[s][:], in_=wg_d[2 * mp:2 * mp + 2].rearrange("m p f -> p m f")),
                     reads=wT, writes=[wgT[s]], dma=True)
                P.op("sp", lambda e, s=s, mp=mp: e.dma_start(out=wus[s][:], in_=wu_d[2 * mp:2 * mp + 2].rearrange("m p f -> p m f")),
                     reads=wT, writes=[wuT[s]], dma=True)
                if t == 0 and mp == 2:
                    load_wd()
                if mp == 5 and t + 1 < NT:
                    xload(t + 1)
                for mi in range(2):
                    m = 2 * mp + mi
                    pg, pgT = banks.get()
                    pu, puT = banks.get()
                    for k in range(NK):
                        P.op("pe", lambda e, k=k, s=s, mi=mi, pg=pg: e.matmul(pg[:], lhsT=wgs[s][:, mi, k * 128:(k + 1) * 128], rhs=h[b][:, k, :],
                                                                           start=(k == 0), stop=(k == NK - 1)),
                             reads=[wgT[s], hT[b]], writes=[pgT], signal=(k == NK - 1))
                    for k in range(NK):
                        P.op("pe", lambda e, k=k, s=s, mi=mi, pu=pu: e.matmul(pu[:], lhsT=wus[s][:, mi, k * 128:(k + 1) * 128], rhs=h[b][:, k, :],
                                                                           start=(k == 0), stop=(k == NK - 1)),
                             reads=[wuT[s], hT[b]], writes=[puT], signal=(k == NK - 1))
                    j = m % 2
                    P.op("act", lambda e, j=j, pg=pg: e.activation(out=sg[j][:], in_=pg[:], func=AF.Silu), reads=[pgT], writes=[sgT[j]])
                    P.op("dve", lambda e, j=j, pu=pu, m=m: e.tensor_tensor(out=a[:, m, :], in0=pu[:], in1=sg[j][:], op=ALU.mult),
                         reads=[puT, sgT[j]], writes=[aT[m]])

        def down(t):
            b = t % 2
            for d in range(NK):
                pf, pfT = banks.get()
                for m in range(NM):
                    P.op("pe", lambda e, m=m, d=d, pf=pf: e.matmul(pf[:], lhsT=wd[:, m, d * 128:(d + 1) * 128], rhs=a[:, m, :],
                                                               start=(m == 0), stop=(m == NM - 1)),
                         reads=[wdT, aT[m]], writes=[pfT], signal=(m == NM - 1))
                P.op("act", lambda e, d=d, pf=pf: e.activation(out=fsb[:, d, :], in_=pf[:], func=AF.Copy), reads=[pfT], writes=[fsbT[d]])
                P.op("act", lambda e, d=d, pf=pf: e.activation(out=sq2[:, d, :], in_=pf[:], func=AF.Square), reads=[pfT], writes=[sq2T[d]])
            ps, psT = banks.get()
            for k in range(NK):
                P.op("pe", lambda e, k=k: e.matmul(ps[:], lhsT=ones_bf, rhs=sq2[:, k, :], start=(k == 0), stop=(k == NK - 1)),
                     reads=[sq2T[k], onesT], writes=[psT], signal=(k == NK - 1))
            rstd_from_ss(P, ps[:], psT, rstd2[:], rstd2T, D)
            for d in range(NK):
                j = d % 2
                P.op("dve", lambda e, d=d, j=j: e.scalar_tensor_tensor(out=tmp[j][:], in0=fsb[:, d, :], scalar=gpost[:, d:d + 1], in1=rstd2[:],
                                                                    op0=ALU.mult, op1=ALU.mult),
                     reads=[fsbT[d], rstd2T, gT], writes=[tmpT[j]])
                P.op("pool", lambda e, d=d, j=j: e.tensor_tensor(out=fsb[:, d, :], in0=tmp[j][:], in1=xt[b][:, d, :], op=ALU.add),
                     reads=[tmpT[j], xtT[b]], writes=[fsbT[d]])
            oT = T()
            outs.append(oT)
            P.op("act", lambda e: e.dma_start(out=dstv[:, :, t * TT:(t + 1) * TT], in_=fsb[:]), reads=fsbT, writes=[oT], dma=True)

        xload(0)
        prologue(0)
        for t in range(NT):
            gateup(t)
            run_hooks(3)
            if t + 1 < NT:
                prologue(t + 1)
            down(t)
        barrier(P)
        return outs


NCH = L // 128
NCH_RUN = NCH
DIN = 5152
C_Z, C_X, C_DT, C_U, C_V = 0, 1024, 3072, 3104, 4128


def load_consts(P, nc, es, consts_d):
    sb = lambda name, shape, dt: es.enter_context(nc.sbuf_tensor(f"{name}_{_uid()}", shape, dt))
    cf = sb("c_f32", [128, 6, 128], F32)
    cb = sb("c_bf", [128, 6, 128], BF16)
    cT = T("consts")
    P.op("sp", lambda e: e.dma_start(out=cf[:], in_=consts_d), writes=[cT], dma=True)
    P.op("dve", lambda e: e.tensor_copy(out=cb[:], in_=cf[:]), reads=[cT], writes=[cT])
    return dict(cf=cf, cb=cb, T=cT, ident_bf=cb[:, 0, :], ones_bf=cb[:, 5, :], ones_f=cf[:, 5, :],
                triF=cf[:, 1, :], triB=cf[:, 2, :], Af=cf[:, 3, :], Ab=cf[:, 4, :],
                triF_bf=cb[:, 1, :], triB_bf=cb[:, 2, :])


def mixer_stage(P, nc, banks, src, dst, prm, scr, C):
    cT = C["T"]
    srcv = src.rearrange("(k p) t -> p k t", p=128)
    dstv = dst.rearrange("(k p) t -> p k t", p=128)
    YP, ZS, YG, CTs, SM, STB = scr["YP"], scr["ZS"], scr["YG"], scr["CT"], scr["SM"], scr["STB"]
    outs = []
    def sweepA():
        with ExitStack() as es:
            sb = lambda name, shape, dt: es.enter_context(nc.sbuf_tensor(f"{name}_{_uid()}", shape, dt))
            OP = P.op
            win = sb("a_win", [128, NK, DIN], BF16)
            wblocks = [(C_DT, C_DT + 32), (C_X, C_X + 1024), (C_X + 1024, C_X + 2048), (C_V, C_V + 1024), (C_Z, C_Z + 1024), (C_U, C_U + 1024)]
            wbT = [T() for _ in wblocks]
            win_v = prm["win_b"].rearrange("(k p) n -> p k n", p=128)
            for (c0, c1), tr in zip(wblocks, wbT):
                OP("sp", lambda e, c0=c0, c1=c1: e.dma_start(out=win[:, :, c0:c1], in_=win_v[:, :, c0:c1]), reads=prm["wT"], writes=[tr], dma=True)

            def winT_of(col):
                for (c0, c1), tr in zip(wblocks, wbT):
                    if c0 <= col < c1:
                        return tr
                raise RuntimeError("col")
            prT = T("params")
            prLs = []
            convw = sb("a_convw", [128, 16, 5], F32); convb = sb("a_convb", [128, 16], F32)
            dtbias = sb("a_dtb", [128, 32], F32); abc = sb("a_abc", [128, 32], F32)
            dsk = sb("a_dsk", [128, 8], F32); gpre = sb("a_gpre", [128, 8], F32)
            lng = sb("a_lng", [128, 1024], F32); lnb = sb("a_lnb", [128, 1024], F32)
            vg = sb("a_vg", [128, 1024], F32); vgT = T()
            swb = sb("a_swb", [128, 8, 128], BF16)
            sbrow = sb("a_sbrow", [1, 1024], F32)
            for t_, d_ in ((convw, "convw"), (convb, "convb"), (dtbias, "dtbias"), (abc, "alog"), (dsk, "dsk"), (gpre, "gpre"),
                           (lng, "lng"), (lnb, "lnb"), (sbrow, "sbrow")):
                prLs.append(T())
                OP("sp", lambda e, t_=t_, d_=d_: e.dma_start(out=t_[:], in_=prm[d_]), writes=[prLs[-1]], dma=True)
            OP("act", lambda e: e.activation(out=abc[:], in_=abc[:], func=AF.Exp), reads=prLs, writes=[prT])
            OP("dve", lambda e: e.tensor_scalar(out=abc[:], in0=abc[:], scalar1=-1.0, scalar2=None, op0=ALU.mult), reads=[prT], writes=[prT])
            OP("sp", lambda e: e.dma_start(out=vg[:].rearrange("p (g i) -> p g i", g=8), in_=prm["swT"]), writes=[vgT], dma=True)
            OP("dve", lambda e: e.tensor_copy(out=swb[:], in_=vg[:].rearrange("p (g i) -> p g i", g=8)), reads=[prT, vgT], writes=[prT])
            cdiag = sb("a_cdiag", [128, 16, 5, 128], BF16)
            cdiagb = sb("a_cdiagb", [128, 16, 128], BF16)
            for q in range(16):
                OP("dve", lambda e, q=q: e.tensor_scalar(out=cdiagb[:, q, :], in0=C["ident_bf"], scalar1=convb[:, q:q + 1],
                                                        scalar2=None, op0=ALU.mult), reads=[prT, cT], writes=[prT])
            for q in range(16):
                for tp in range(5):
                    OP("dve", lambda e, q=q, tp=tp: e.tensor_scalar(out=cdiag[:, q, tp, :], in0=C["ident_bf"], scalar1=convw[:, q, tp:tp + 1],
                                                                  scalar2=None, op0=ALU.mult), reads=[prT, cT], writes=[prT])

            xh = sb("a_xh", [128, NK, 132], F32); xhT = T()
            sq = sb("a_sq", [128, NK, 132], BF16); sqT = T()
            rstd = sb("a_rstd", [128, 132], F32); rstdT = T()
            gT_ = [sb(f"a_gT{i}", [128, NK, 132], BF16) for i in range(2)]; gTT = [T() for _ in range(2)]
            xbc = sb("a_xbc", [128, 16, 132], BF16); xbcT = [T() for _ in range(6)]
            xc = [sb(f"a_xc{i}", [128, 16, 128], BF16) for i in range(2)]; xcT = [[T() for _ in range(4)] for _ in range(2)]
            zs = sb("a_zs", [128, 8, 128], BF16); zsT = T()
            ug = sb("a_ug", [128, 8, 128], BF16); ugT = T()
            vb = sb("a_vb", [128, 1024], BF16); vbT = T()
            ygm = sb("a_ygm", [128, 8, 128], BF16); ygmT = T()
            bst = sb("a_bst", [128, 2, 6], F32); mv = sb("a_mv", [128, 2], F32); rsv = sb("a_rsv", [128, 1], F32); lnT = T()
            smalls = [{n: sb(f"a_{n}{i}", [128, 32], F32) for n in ("dtr", "dt", "dta", "cs_sb", "ecs", "dec", "sm", "etot", "dtd")} for i in range(2)]
            smallTs = [T() for _ in range(2)]; smTs = [T() for _ in range(2)]
            xdt = [sb(f"a_xdt{i}", [128, 1024], BF16) for i in range(4)]; xdtT = [T() for _ in range(4)]
            btok = sb("a_btok", [128, 512], BF16); btokT = T()
            cbm = [sb(f"a_cbm{i}", [128, 512], BF16) for i in range(2)]; cbmT = [T() for _ in range(2)]
            dtri2 = [sb(f"a_dtri2{i}", [128, 2, 4, 128], F32) for i in range(2)]; dtri2T = [T() for _ in range(2)]
            Eb = [sb(f"a_E{i}", [128, 512], BF16) for i in range(2)]; EbT = [T() for _ in range(2)]
            Mb = [sb(f"a_M{i}", [128, 512], BF16) for i in range(4)]; MbT = [T() for _ in range(4)]
            yoff = sb("a_yoff", [128, 1024], BF16); yoffT = T()
            Sf = sb("a_Sf", [128, 1024], F32); SfT = T()
            Sbf = sb("a_Sbf", [128, 1024], BF16); SbfT = T()
            ypart = sb("a_ypart", [128, 8, 128], F32); ypT = T()
            stb = sb("a_stb", [128, 1024], F32); stbT = T()
            stf = sb("a_stf", [128, 1024], F32); stfT = T()
            OP("dve", lambda e: e.memset(Sf[:], 0.0), writes=[SfT])
            OP("dve", lambda e: e.memset(Sbf[:], 0.0), writes=[SbfT])

            psy = banks.reserve(2)
            segbanks = banks.reserve(3)
            rot_base = list(banks.rot)
            seg_ids = [i for i in range(8) if any(banks.t[i] is sbk[0] for sbk in segbanks)]

            def norm_a(c):
                lo, hi = 128 * c - 2, 128 * c + 130
                if c == 0:
                    OP("dve", lambda e: e.memset(xh[:, :, 0:2], 0.0), writes=[xhT])
                    OP("sp", lambda e: e.dma_start(out=xh[:, :, 2:132], in_=srcv[:, :, 0:130]), writes=[xhT], dma=True)
                elif c == NCH - 1:
                    OP("dve", lambda e: e.memset(xh[:, :, 130:132], 0.0), writes=[xhT])
                    OP("sp", lambda e, lo=lo: e.dma_start(out=xh[:, :, 0:130], in_=srcv[:, :, lo:L]), writes=[xhT], dma=True)
                else:
                    OP("sp", lambda e, lo=lo, hi=hi: e.dma_start(out=xh[:, :, :], in_=srcv[:, :, lo:hi]), writes=[xhT], dma=True)
                OP("act", lambda e: e.activation(out=sq[:], in_=xh[:], func=AF.Square), reads=[xhT], writes=[sqT])

            def norm_b(c):
                b = c % 2
                g_ = gT_[b]; gt = gTT[b]
                ps, psT = banks.get()
                for k in range(NK):
                    OP("pe", lambda e, k=k, ps=ps: e.matmul(ps[:, 0:132], lhsT=C["ones_bf"], rhs=sq[:, k, :], start=(k == 0), stop=(k == NK - 1)),
                       reads=[sqT, cT], writes=[psT], signal=(k == NK - 1))
                rstd_from_ss(P, ps[:, 0:132], psT, rstd[:], rstdT, D)
                for k in range(NK):
                    OP("dve", lambda e, k=k, g_=g_: e.scalar_tensor_tensor(out=g_[:, k, :], in0=xh[:, k, :], scalar=gpre[:, k:k + 1], in1=rstd[:],
                                                                        op0=ALU.mult, op1=ALU.mult), reads=[xhT, rstdT, prT], writes=[gt])

            def front(c):
                b = c % 2
                g_ = gT_[b]; gt = gTT[b]; xcc = xc[b]; xct = xcT[b]
                sm_ = smalls[b]; smallT = smallTs[b]; smT = smTs[b]
                dtr, dt, dta, cs_sb, ecs, dec, sm, etot, dtd = (sm_[n] for n in ("dtr", "dt", "dta", "cs_sb", "ecs", "dec", "sm", "etot", "dtd"))

                def proj(psl, col, n0, n1, last, g_=g_, gt=gt):
                    for k in range(NK):
                        OP("pe", lambda e, k=k: e.matmul(psl, lhsT=win[:, k, col:col + 128], rhs=g_[:, k, n0:n1], start=(k == 0), stop=(k == NK - 1)),
                           reads=[winT_of(col), gt], writes=[last[1]], signal=(k == NK - 1 and last[0]))

                ps, psT = banks.get()
                for k in range(NK):
                    OP("pe", lambda e, k=k, ps=ps, g_=g_: e.matmul(ps[:, 0:32], lhsT=g_[:, k, 2:130], rhs=win[:, k, C_DT:C_DT + 32], start=(k == 0), stop=(k == NK - 1)),
                       reads=[winT_of(C_DT), gt], writes=[psT], signal=(k == NK - 1))
                OP("dve", lambda e, ps=ps: e.tensor_tensor(out=dtr[:], in0=ps[:, 0:32], in1=dtbias[:], op=ALU.add), reads=[psT, prT], writes=[smallT])
                OP("act", lambda e: e.activation(out=dtr[:], in_=dtr[:], func=AF.Exp), reads=[smallT], writes=[smallT])
                OP("act", lambda e: e.activation(out=dt[:], in_=dtr[:], func=AF.Ln, bias=1.0), reads=[smallT], writes=[smallT])
                OP("dve", lambda e: e.tensor_tensor(out=dta[:], in0=dt[:], in1=abc[:], op=ALU.mult), reads=[smallT, prT], writes=[smallT])
                yield
                for bi in range(6):
                    qs = list(range(3 * bi, min(16, 3 * bi + 3)))
                    ps, psT = banks.get()
                    for j, q in enumerate(qs):
                        proj(ps[:, j * 132:(j + 1) * 132], C_X + 128 * q, 0, 132, (j == len(qs) - 1, psT))
                    n = len(qs)
                    OP("act", lambda e, ps=ps, q0=qs[0], n=n: e.activation(out=xbc[:, q0:q0 + n, :], in_=ps[:, 0:n * 132].rearrange("p (q t) -> p q t", q=n),
                                                                      func=AF.Copy), reads=[psT], writes=[xbcT[bi]])
                    if bi == 0 and c + 1 < NCH:
                        norm_a(c + 1)
                    if bi == 4 and c + 1 < NCH:
                        norm_b(c + 1)
                    yield
                yield "P"
                for hf in range(2):
                    ps, psT = banks.get()
                    for k in range(NK):
                        OP("pe", lambda e, k=k, ps=ps, hf=hf, g_=g_: e.matmul(ps[:], lhsT=g_[:, k, 2:130], rhs=win[:, k, C_V + 512 * hf:C_V + 512 * hf + 512],
                                                                          start=(k == 0), stop=(k == NK - 1)),
                           reads=[winT_of(C_V), gt], writes=[psT], signal=(k == NK - 1))
                    OP("act", lambda e, ps=ps, hf=hf: e.activation(out=vg[:, 512 * hf:512 * hf + 512], in_=ps[:], func=AF.Gelu), reads=[psT], writes=[vgT])
                    OP("dve", lambda e, hf=hf: e.bn_stats(out=bst[:, hf, :], in_=vg[:, 512 * hf:512 * hf + 512]), reads=[vgT], writes=[lnT])
                    yield
                OP("dve", lambda e: e.bn_aggr(out=mv[:], in_=bst[:].rearrange("p a b -> p (a b)")), reads=[lnT], writes=[lnT])
                OP("act", lambda e: e.activation(out=rsv[:], in_=mv[:, 1:2], func=AF.Ln, bias=EPS), reads=[lnT], writes=[lnT])
                OP("act", lambda e: e.activation(out=rsv[:], in_=rsv[:], func=AF.Exp, scale=-0.5), reads=[lnT], writes=[lnT])
                OP("dve", lambda e: e.tensor_scalar(out=vg[:], in0=vg[:], scalar1=mv[:, 0:1], scalar2=rsv[:, 0:1], op0=ALU.subtract, op1=ALU.mult),
                   reads=[vgT, lnT], writes=[vgT])
                OP("dve", lambda e: e.tensor_tensor(out=vg[:], in0=vg[:], in1=lng[:], op=ALU.mult), reads=[vgT, prT], writes=[vgT])
                OP("dve", lambda e: e.tensor_tensor(out=vb[:], in0=vg[:], in1=lnb[:], op=ALU.add), reads=[vgT, prT], writes=[vbT])
                yield
                for bi in range(4):
                    ps, psT = banks.get()
                    for j in range(4):
                        q = 4 * bi + j
                        for tp in range(5):
                            OP("pe", lambda e, q=q, tp=tp, j=j, ps=ps: e.matmul(ps[:, j * 128:(j + 1) * 128], lhsT=cdiag[:, q, tp, :], rhs=xbc[:, q, tp:tp + 128],
                                                                              start=(tp == 0), stop=False),
                               reads=[prT, xbcT[q // 3]], writes=[psT], signal=False)
                        OP("pe", lambda e, q=q, j=j, ps=ps: e.matmul(ps[:, j * 128:(j + 1) * 128], lhsT=cdiagb[:, q, :], rhs=C["ones_bf"], start=False, stop=True),
                           reads=[prT, cT], writes=[psT], signal=(j == 3))
                    OP("act", lambda e, bi=bi, ps=ps, xcc=xcc: e.activation(out=xcc[:, 4 * bi:4 * bi + 4, :], in_=ps[:].rearrange("p (q t) -> p q t", q=4), func=AF.Silu),
                       reads=[psT], writes=[xct[bi]])
                    yield
                yield "P"
                ps, psT = banks.get()
                OP("pe", lambda e, ps=ps: e.matmul(ps[:, 0:16], lhsT=C["triF"], rhs=dta[:, 0:16], start=True, stop=True), reads=[cT, smallT], writes=[psT], signal=False)
                OP("pe", lambda e, ps=ps: e.matmul(ps[:, 16:32], lhsT=C["triB"], rhs=dta[:, 16:32], start=True, stop=True), reads=[cT, smallT], writes=[psT], signal=False)
                OP("pe", lambda e, ps=ps: e.matmul(ps[:, 32:64], lhsT=C["ones_f"], rhs=dta[:, 0:32], start=True, stop=True), reads=[cT, smallT], writes=[psT])
                OP("act", lambda e, ps=ps: e.activation(out=cs_sb[:], in_=ps[:, 0:32], func=AF.Copy), reads=[psT], writes=[smallT])
                OP("act", lambda e, ps=ps: e.activation(out=ecs[:], in_=ps[:, 0:32], func=AF.Exp), reads=[psT], writes=[smallT])
                OP("act", lambda e, ps=ps: e.activation(out=etot[:], in_=ps[:, 32:64], func=AF.Exp), reads=[psT], writes=[smallT])
                OP("dve", lambda e, ps=ps: e.tensor_tensor(out=dec[:], in0=ps[:, 32:64], in1=cs_sb[:], op=ALU.subtract), reads=[psT, smallT], writes=[smallT])
                OP("act", lambda e: e.activation(out=dec[:], in_=dec[:], func=AF.Exp), reads=[smallT], writes=[smallT])
                OP("dve", lambda e: e.tensor_tensor(out=dtd[:], in0=dt[:], in1=dec[:], op=ALU.mult), reads=[smallT], writes=[smallT])
                OP("dve", lambda e: e.tensor_copy(out=sm[:, 0:16], in_=ecs[:, 16:32]), reads=[smallT], writes=[smT])
                OP("dve", lambda e: e.tensor_copy(out=sm[:, 16:32], in_=etot[:, 16:32]), reads=[smallT], writes=[smT])
                yield
                for (col0, dstt, dT, fn) in ((C_Z, zs, zsT, AF.Silu), (C_U, ug, ugT, AF.Gelu)):
                    for bi in range(2):
                        ps, psT = banks.get()
                        for j in range(4):
                            proj(ps[:, j * 128:(j + 1) * 128], col0 + 128 * (4 * bi + j), 2, 130, (j == 3, psT))
                        OP("act", lambda e, ps=ps, bi=bi, dstt=dstt, fn=fn: e.activation(out=dstt[:, 4 * bi:4 * bi + 4, :],
                                                                                   in_=ps[:].rearrange("p (q t) -> p q t", q=4), func=fn),
                           reads=[psT], writes=[dT])
                        yield
                for bi in range(2):
                    ps, psT = banks.get()
                    for j in range(4):
                        g = 4 * bi + j
                        OP("pe", lambda e, g=g, j=j, ps=ps: e.matmul(ps[:, j * 128:(j + 1) * 128], lhsT=vb[:, g * 128:(g + 1) * 128], rhs=swb[:, g, :], start=True, stop=False),
                           reads=[vbT, prT], writes=[psT], signal=False)
                        OP("pe", lambda e, g=g, j=j, ps=ps: e.matmul(ps[:, j * 128:(j + 1) * 128], lhsT=C["ones_f"][0:1, :], rhs=sbrow[0:1, g * 128:(g + 1) * 128], start=False, stop=True),
                           reads=[cT, prT], writes=[psT], signal=(j == 3))
                    OP("dve", lambda e, bi=bi, ps=ps: e.tensor_tensor(out=ygm[:, 4 * bi:4 * bi + 4, :], in0=ps[:].rearrange("p (q t) -> p q t", q=4),
                                                                  in1=ug[:, 4 * bi:4 * bi + 4, :], op=ALU.mult), reads=[psT, ugT], writes=[ygmT])
                    yield
                OP("sp", lambda e, c=c: e.dma_start(out=ZS[c], in_=zs[:].rearrange("p q t -> p (q t)")), reads=[zsT], writes=[T()], dma=True)
                OP("sp", lambda e, c=c: e.dma_start(out=YG[c], in_=ygm[:].rearrange("p q t -> p (q t)")), reads=[ygmT], writes=[T()], dma=True)
                OP("sp", lambda e, c=c: e.dma_start(out=SM[c], in_=sm[:]), reads=[smT], writes=[T()], dma=True)
                yield

            def back(c):
                b = c % 2
                xcc = xc[b]; xct = xcT[b]
                sm_ = smalls[b]; smallT = smallTs[b]; smT = smTs[b]
                dtr, dt, dta, cs_sb, ecs, dec, sm, etot, dtd = (sm_[n] for n in ("dtr", "dt", "dta", "cs_sb", "ecs", "dec", "sm", "etot", "dtd"))
                dirs = ((C["Af"], C["triF"], 0), (C["Ab"], C["triB"], 16))

                def dtri_issue(g):
                    gb = g % 2
                    OP("dve", lambda e, gb=gb, g=g: e.tensor_tensor(out=dtri2[gb][:], in0=C["cf"][:, 1:3, :].unsqueeze(2).to_broadcast([128, 2, 4, 128]),
                                                                 in1=dta[:].rearrange("p (d h) -> p d h", d=2)[:, :, 4 * g:4 * g + 4].unsqueeze(3).to_broadcast([128, 2, 4, 128]),
                                                                 op=ALU.mult),
                       reads=[cT, smallT], writes=[dtri2T[gb]])

                dtri_issue(0)
                dtri_issue(1)
                ps, psT = banks.get()
                psb = ps[:].bitcast(BF16)
                for q in range(8):
                    OP("pe", lambda e, q=q, psb=psb, xcc=xcc: e.transpose(out=psb[:, q * 128:(q + 1) * 128], in_=xcc[:, q, :], identity=C["ident_bf"]),
                       reads=[xct[q // 4], cT], writes=[psT], signal=(q == 7))
                for i, scl in enumerate((dt[:, 0:16], dt[:, 16:32], dtd[:, 0:16], dtd[:, 16:32])):
                    OP("dve", lambda e, i=i, scl=scl, psb=psb: e.tensor_tensor(out=xdt[i][:].rearrange("p (h d) -> p h d", h=16), in0=psb.rearrange("p (h d) -> p h d", h=16),
                                                                             in1=scl.unsqueeze(2).to_broadcast([128, 16, 64]), op=ALU.mult),
                       reads=[psT, smallT], writes=[xdtT[i]])
                yield
                ps, psT = banks.get()
                psb2 = ps[:].bitcast(BF16)
                for q in range(4):
                    OP("pe", lambda e, q=q, psb2=psb2, xcc=xcc: e.transpose(out=psb2[:, q * 128:(q + 1) * 128], in_=xcc[:, 8 + q, :], identity=C["ident_bf"]),
                       reads=[xct[2], cT], writes=[psT], signal=(q == 3))
                OP("act", lambda e, psb2=psb2: e.activation(out=btok[:], in_=psb2[:, 0:512], func=AF.Copy), reads=[psT], writes=[btokT])
                yield
                ps, psT = banks.get()
                for g in range(4):
                    OP("pe", lambda e, g=g, ps=ps, xcc=xcc: e.matmul(ps[:, g * 128:(g + 1) * 128], lhsT=xcc[:, 8 + g, :], rhs=xcc[:, 12 + g, :], start=True, stop=True),
                       reads=[xct[2], xct[3]], writes=[psT], signal=(g == 3))
                for i, tri in enumerate((C["triF_bf"], C["triB_bf"])):
                    OP("dve", lambda e, i=i, tri=tri, ps=ps: e.tensor_tensor(out=cbm[i][:].rearrange("p (g t) -> p g t", g=4), in0=ps[:].rearrange("p (g t) -> p g t", g=4),
                                                                           in1=tri.unsqueeze(1).to_broadcast([128, 4, 128]), op=ALU.mult),
                       reads=[psT, cT], writes=[cbmT[i]])
                yield
                if c > 0:
                    for hf in range(2):
                        ps, psT = banks.get()
                        for j in range(2):
                            g = 2 * hf + j
                            OP("pe", lambda e, g=g, j=j, ps=ps, xcc=xcc: e.matmul(ps[:, j * 256:(j + 1) * 256], lhsT=xcc[:, 12 + g, :], rhs=Sbf[:, g * 256:(g + 1) * 256], start=True, stop=True),
                               reads=[xct[3], SbfT], writes=[psT], signal=(j == 1))
                        OP("dve", lambda e, hf=hf, ps=ps: e.tensor_tensor(out=yoff[:, 512 * hf:512 * hf + 512].rearrange("p (h d) -> p h d", h=8),
                                                                      in0=ps[:].rearrange("p (h d) -> p h d", h=8),
                                                                      in1=ecs[:, 8 * hf:8 * hf + 8].unsqueeze(2).to_broadcast([128, 8, 64]), op=ALU.mult),
                           reads=[psT, smallT], writes=[yoffT])
                        yield
                yield "P"
                segps = {}

                def seg_issue(s_):
                    g, di = s_ // 2, s_ % 2
                    Amat, tri, off = dirs[di]
                    gb = g % 2
                    ps, psT = segbanks[s_ % 3]
                    OP("pe", lambda e, ps=ps, Amat=Amat, gb=gb, di=di: e.matmul(ps[:], lhsT=Amat, rhs=dtri2[gb][:, di, :, :].rearrange("p r t -> p (r t)"), start=True, stop=True),
                       reads=[cT, dtri2T[gb]], writes=[psT])
                    segps[s_] = (ps, psT)

                def em_issue(s_):
                    g, di = s_ // 2, s_ % 2
                    ps, psT = segps[s_]
                    jt = s_ % 2
                    OP("act", lambda e, ps=ps, jt=jt: e.activation(out=Eb[jt][:], in_=ps[:], func=AF.Exp), reads=[psT], writes=[EbT[jt]])
                    mi = s_ % 4
                    OP("pool", lambda e, mi=mi, jt=jt, di=di, g=g: e.tensor_tensor(out=Mb[mi][:].rearrange("p (r t) -> p r t", r=4), in0=Eb[jt][:].rearrange("p (r t) -> p r t", r=4),
                                                                               in1=cbm[di][:, g * 128:(g + 1) * 128].unsqueeze(1).to_broadcast([128, 4, 128]), op=ALU.mult),
                       reads=[EbT[jt], cbmT[di]], writes=[MbT[mi]])

                def y_issue(g):
                    Ms = [(2 * g) % 4, (2 * g + 1) % 4]
                    for r in range(4):
                        h = 4 * g + r
                        q = h // 2
                        pr = (h % 2) * 64
                        psyt, psyT = psy[q // 4]
                        outap = psyt[pr:pr + 64, (q % 4) * 128:(q % 4) * 128 + 128]
                        OP("pe", lambda e, outap=outap, h=h, r=r, m=Ms[0]: e.matmul(outap, lhsT=xdt[0][:, h * 64:(h + 1) * 64], rhs=Mb[m][:, r * 128:(r + 1) * 128], start=True, stop=False),
                           reads=[xdtT[0], MbT[Ms[0]]], writes=[psyT], signal=False)
                        OP("pe", lambda e, outap=outap, h=h, r=r, m=Ms[1]: e.matmul(outap, lhsT=xdt[1][:, h * 64:(h + 1) * 64], rhs=Mb[m][:, r * 128:(r + 1) * 128], start=False, stop=(c == 0)),
                           reads=[xdtT[1], MbT[Ms[1]]], writes=[psyT], signal=(c == 0 and (r == 3)))
                        if c > 0:
                            OP("pe", lambda e, outap=outap, h=h: e.matmul(outap, lhsT=yoff[:, h * 64:(h + 1) * 64], rhs=C["ident_bf"], start=False, stop=True),
                               reads=[yoffT, cT], writes=[psyT], signal=(r == 3))

                seg_issue(0)
                seg_issue(1)
                yield
                for s_ in range(8):
                    if s_ == 1:
                        dtri_issue(2)
                    if s_ == 3:
                        dtri_issue(3)
                    if s_ + 2 < 8:
                        seg_issue(s_ + 2)
                    em_issue(s_)
                    if s_ >= 3 and s_ % 2 == 1:
                        y_issue((s_ - 3) // 2)
                    yield
                y_issue(3)
                yield "P"
                for q in range(8):
                    psyt, psyT = psy[q // 4]
                    OP("dve", lambda e, q=q, psyt=psyt, xcc=xcc: e.scalar_tensor_tensor(out=ypart[:, q, :], in0=xcc[:, q, :], scalar=dsk[:, q:q + 1],
                                                                                    in1=psyt[:, (q % 4) * 128:(q % 4) * 128 + 128], op0=ALU.mult, op1=ALU.add),
                       reads=[xct[q // 4], prT, psyT], writes=[ypT])
                OP("sp", lambda e, c=c: e.dma_start(out=YP[c], in_=ypart[:].rearrange("p q t -> p (q t)")), reads=[ypT], writes=[T()], dma=True)
                yield
                for di in range(2):
                    pss = [banks.get() for _ in range(2)]
                    for g in range(4):
                        pst, pstT = pss[g // 2]
                        OP("pe", lambda e, g=g, pst=pst, di=di: e.matmul(pst[:, (g % 2) * 256:(g % 2) * 256 + 256], lhsT=btok[:, g * 128:(g + 1) * 128], rhs=xdt[2 + di][:, g * 256:(g + 1) * 256],
                                                                      start=True, stop=True), reads=[btokT, xdtT[2 + di]], writes=[pstT], signal=(g % 2 == 1))
                    if di == 0:
                        OP("dve", lambda e: e.tensor_tensor(out=Sf[:].rearrange("p (h d) -> p h d", h=16), in0=Sf[:].rearrange("p (h d) -> p h d", h=16),
                                                            in1=etot[:, 0:16].unsqueeze(2).to_broadcast([128, 16, 64]), op=ALU.mult), reads=[SfT, smallT], writes=[SfT])
                        for hf in range(2):
                            OP("act", lambda e, hf=hf, pst=pss[hf][0]: e.activation(out=stf[:, 512 * hf:512 * hf + 512], in_=pst[:], func=AF.Copy), reads=[pss[hf][1]], writes=[stfT])
                        OP("dve", lambda e: e.tensor_tensor(out=Sf[:], in0=Sf[:], in1=stf[:], op=ALU.add), reads=[SfT, stfT], writes=[SfT])
                        OP("act", lambda e: e.activation(out=Sbf[:], in_=Sf[:], func=AF.Copy), reads=[SfT], writes=[SbfT])
                    else:
                        for hf in range(2):
                            OP("act", lambda e, hf=hf, pst=pss[hf][0]: e.activation(out=stb[:, 512 * hf:512 * hf + 512], in_=pst[:], func=AF.Copy), reads=[pss[hf][1]], writes=[stbT])
                    yield
                OP("sp", lambda e, c=c, xcc=xcc: e.dma_start(out=CTs[c], in_=xcc[:, 12:16, :].rearrange("p q t -> p (q t)")), reads=[xct[3]], writes=[T()], dma=True)
                OP("sp", lambda e, c=c: e.dma_start(out=STB[c], in_=stb[:]), reads=[stbT], writes=[T()], dma=True)
            def interleave(*gens):
                gens = [g for g in gens if g is not None]
                while gens:
                    for g in list(gens):
                        try:
                            next(g)
                        except StopIteration:
                            gens.remove(g)

            def run(gens, lead=0):
                live = list(gens)
                for _ in range(lead):
                    try:
                        if next(live[0]) == "P":
                            live.pop(0)
                            break
                    except StopIteration:
                        live.pop(0)
                        break
                while live:
                    for g in list(live):
                        try:
                            if next(g) == "P":
                                live.remove(g)
                        except StopIteration:
                            live.remove(g)

            norm_a(0)
            norm_b(0)
            bprev = None
            for c in range(NCH):
                f = front(c)
                banks.rot = list(rot_base)
                run([f] + ([bprev] if bprev is not None else []))
                banks.rot = list(rot_base) + seg_ids
                run([f] + ([bprev] if bprev is not None else []))
                bcur = back(c)
                run([f, bcur], lead=2)
                if c >= 1:
                    run_hooks(1)
                bprev = bcur
            run([bprev]); run([bprev])
            banks.release()
            barrier(P)

    def sweepB():
        with ExitStack() as es:
            sb = lambda name, shape, dt: es.enter_context(nc.sbuf_tensor(f"{name}_{_uid()}", shape, dt))
            OP = P.op
            wout = sb("b_wout", [128, 16, D], BF16); woutT = T()
            for k4 in range(0, 16, 4):
                OP("sp", lambda e, k4=k4: e.dma_start(out=wout[:, k4:k4 + 4, :], in_=prm["wout_b"][k4 * 128:(k4 + 4) * 128, :].rearrange("(k p) n -> p k n", p=128)),
                   reads=prm["wT"], writes=[woutT], dma=True)
            prT = T()
            ssdg = sb("b_ssdg", [128, 8], F32); gpost = sb("b_gpost", [128, 8], F32)
            OP("sp", lambda e: e.dma_start(out=ssdg[:], in_=prm["ssdg"]), writes=[prT], dma=True)
            OP("sp", lambda e: e.dma_start(out=gpost[:], in_=prm["gpost"]), writes=[prT], dma=True)
            NB = 4
            yp = [sb(f"b_yp{i}", [128, 8, 128], F32) for i in range(NB)]
            zs = [sb(f"b_zs{i}", [128, 8, 128], BF16) for i in range(NB)]
            yg4 = [sb(f"b_yg4{i}", [128, 8, 512], BF16) for i in range(3)]; yg4T = [[T() for _ in range(4)] for _ in range(3)]
            ct = [sb(f"b_ct{i}", [128, 4, 128], BF16) for i in range(NB)]
            sm = [sb(f"b_sm{i}", [128, 32], F32) for i in range(NB)]
            stb = [sb(f"b_stb{i}", [128, 1024], F32) for i in range(NB)]
            xin4 = sb("b_xin4", [128, 8, 512], F32); xin4T = T()
            ldT = [[T() for _ in range(6)] for _ in range(NB)]
            Sb = sb("b_S", [128, 1024], F32); SbT = T()
            Sbb_ = [sb(f"b_Sbf{i}", [128, 1024], BF16) for i in range(2)]; SbbT_ = [T() for _ in range(2)]
            yoff_ = [sb(f"b_yoff{i}", [128, 1024], BF16) for i in range(2)]; yoffT_ = [T() for _ in range(2)]
            y_ = [sb(f"b_y{i}", [128, 8, 128], F32) for i in range(2)]; yT_ = [T() for _ in range(2)]
            sq_ = [sb(f"b_sq{i}", [128, 8, 128], BF16) for i in range(2)]; sqT_ = [T() for _ in range(2)]
            rstd_ = [sb(f"b_rstd{i}", [128, 128], F32) for i in range(2)]; rstdT_ = [T() for _ in range(2)]
            yssd4 = [sb(f"b_yssd4{i}", [128, 8, 512], BF16) for i in range(2)]; yssd4T = [[T() for _ in range(4)] for _ in range(2)]
            sq2 = sb("b_sq2", [128, 8, 512], BF16); sq2T = [T() for _ in range(8)]
            fsb = sb("b_fsb", [128, 8, 512], F32); fsbT = [T() for _ in range(8)]
            rstd2 = sb("b_rstd2", [128, 512], F32); rstd2T = T()
            tmp = [sb(f"b_tmp{i}", [128, 512], F32) for i in range(2)]; tmpT = [T() for _ in range(2)]
            OP("dve", lambda e: e.memset(Sb[:], 0.0), writes=[SbT])

            def loads(c):
                b = c % NB
                OP("sp", lambda e: e.dma_start(out=yp[b][:].rearrange("p q t -> p (q t)"), in_=YP[c]), writes=[ldT[b][0]], dma=True)
                OP("sp", lambda e: e.dma_start(out=zs[b][:].rearrange("p q t -> p (q t)"), in_=ZS[c]), writes=[ldT[b][1]], dma=True)
                tl, j = c // 4, c % 4
                OP("sp", lambda e: e.dma_start(out=yg4[tl % 3][:, :, j * 128:(j + 1) * 128], in_=YG[c].rearrange("p (q t) -> p q t", q=8)), writes=[yg4T[tl % 3][j]], dma=True)
                OP("sp", lambda e: e.dma_start(out=ct[b][:].rearrange("p q t -> p (q t)"), in_=CTs[c]), writes=[ldT[b][3]], dma=True)
                OP("sp", lambda e: e.dma_start(out=sm[b][:], in_=SM[c]), writes=[ldT[b][4]], dma=True)
                OP("sp", lambda e: e.dma_start(out=stb[b][:], in_=STB[c]), writes=[ldT[b][5]], dma=True)

            order = list(range(NCH_RUN - 1, -1, -1))
            loads(order[0])
            def b1a(ci, c):
                b = c % NB
                if ci + 1 < len(order):
                    loads(order[ci + 1])
                lt = ldT[b]
                yoff = yoff_[ci % 2]; yoffT = yoffT_[ci % 2]
                if ci > 0:
                    Sbb = Sbb_[ci % 2]; SbbT = SbbT_[ci % 2]
                    for hf in range(2):
                        ps, psT = banks.get()
                        for j in range(2):
                            g = 2 * hf + j
                            OP("pe", lambda e, g=g, j=j, ps=ps: e.matmul(ps[:, j * 256:(j + 1) * 256], lhsT=ct[b][:, g, :], rhs=Sbb[:, g * 256:(g + 1) * 256], start=True, stop=True),
                               reads=lt + [SbbT], writes=[psT], signal=(j == 1))
                        OP("dve", lambda e, hf=hf, ps=ps: e.tensor_tensor(out=yoff[:, 512 * hf:512 * hf + 512].rearrange("p (h d) -> p h d", h=8),
                                                                      in0=ps[:].rearrange("p (h d) -> p h d", h=8),
                                                                      in1=sm[b][:, 8 * hf:8 * hf + 8].unsqueeze(2).to_broadcast([128, 8, 64]), op=ALU.mult),
                           reads=[psT] + lt, writes=[yoffT])
                        yield
                if ci + 1 < len(order):
                    nS = Sbb_[(ci + 1) % 2]; nST = SbbT_[(ci + 1) % 2]
                    OP("dve", lambda e: e.tensor_tensor(out=Sb[:].rearrange("p (h d) -> p h d", h=16), in0=Sb[:].rearrange("p (h d) -> p h d", h=16),
                                                        in1=sm[b][:, 16:32].unsqueeze(2).to_broadcast([128, 16, 64]), op=ALU.mult), reads=[SbT] + lt, writes=[SbT])
                    OP("dve", lambda e: e.tensor_tensor(out=Sb[:], in0=Sb[:], in1=stb[b][:], op=ALU.add), reads=[SbT] + lt, writes=[SbT])
                    OP("act", lambda e: e.activation(out=nS[:], in_=Sb[:], func=AF.Copy), reads=[SbT], writes=[nST])
                    yield

            def b1b(ci, c):
                b = c % NB
                lt = ldT[b]
                yoff = yoff_[ci % 2]; yoffT = yoffT_[ci % 2]
                y = y_[ci % 2]; yT = yT_[ci % 2]; sq = sq_[ci % 2]; sqT = sqT_[ci % 2]
                if ci > 0:
                    for hf in range(2):
                        ps, psT = banks.get()
                        for j in range(4):
                            q = 4 * hf + j
                            OP("pe", lambda e, q=q, j=j, ps=ps: e.matmul(ps[:, j * 128:(j + 1) * 128], lhsT=yoff[:, q * 128:(q + 1) * 128], rhs=C["ident_bf"], start=True, stop=True),
                               reads=[yoffT, cT], writes=[psT], signal=(j == 3))
                        OP("dve", lambda e, hf=hf, ps=ps: e.tensor_tensor(out=y[:, 4 * hf:4 * hf + 4, :], in0=ps[:].rearrange("p (q t) -> p q t", q=4), in1=yp[b][:, 4 * hf:4 * hf + 4, :], op=ALU.add),
                           reads=[psT] + lt, writes=[yT])
                        yield
                    OP("dve", lambda e: e.tensor_tensor(out=y[:], in0=y[:], in1=zs[b][:], op=ALU.mult), reads=[yT] + lt, writes=[yT])
                else:
                    OP("dve", lambda e: e.tensor_tensor(out=y[:], in0=yp[b][:], in1=zs[b][:], op=ALU.mult), reads=lt, writes=[yT])
                OP("act", lambda e: e.activation(out=sq[:], in_=y[:], func=AF.Square), reads=[yT], writes=[sqT])
                yield

            def b15(ci, c):
                y = y_[ci % 2]; yT = yT_[ci % 2]; sq = sq_[ci % 2]; sqT = sqT_[ci % 2]
                rstd = rstd_[ci % 2]; rstdT = rstdT_[ci % 2]
                tl, j = c // 4, c % 4
                yssd = yssd4[tl % 2]; yssdT = yssd4T[tl % 2][j]
                ps, psT = banks.get()
                for k in range(NK):
                    OP("pe", lambda e, k=k, ps=ps: e.matmul(ps[:, 0:128], lhsT=C["ones_bf"], rhs=sq[:, k, :], start=(k == 0), stop=(k == NK - 1)),
                       reads=[sqT, cT], writes=[psT], signal=(k == NK - 1))
                rstd_from_ss(P, ps[:, 0:128], psT, rstd[:], rstdT, D)
                OP("pool", lambda e: e.tensor_tensor(out=y[:], in0=y[:], in1=rstd[:].unsqueeze(1).to_broadcast([128, 8, 128]), op=ALU.mult), reads=[yT, rstdT], writes=[yT])
                OP("pool", lambda e: e.tensor_tensor(out=yssd[:, :, j * 128:(j + 1) * 128], in0=y[:], in1=ssdg[:].unsqueeze(2).to_broadcast([128, 8, 128]), op=ALU.mult),
                   reads=[yT, prT], writes=[yssdT])
                yield

            def b2(tl):
                yssd = yssd4[tl % 2]; yssdTs = yssd4T[tl % 2]
                ygt = yg4[tl % 3]; ygTs = yg4T[tl % 3]
                OP("sp", lambda e: e.dma_start(out=xin4[:], in_=srcv[:, :, tl * 512:(tl + 1) * 512]), writes=[xin4T], dma=True)
                for d in range(8):
                    pst, pstT = banks.get()
                    for k in range(16):
                        rhs = yssd[:, k, :] if k < 8 else ygt[:, k - 8, :]
                        OP("pe", lambda e, d=d, k=k, rhs=rhs, pst=pst: e.matmul(pst[:], lhsT=wout[:, k, d * 128:(d + 1) * 128], rhs=rhs, start=(k == 0), stop=(k == 15)),
                           reads=[woutT] + yssdTs + ygTs, writes=[pstT], signal=(k == 15))
                    OP("act", lambda e, d=d, pst=pst: e.activation(out=fsb[:, d, :], in_=pst[:], func=AF.Copy), reads=[pstT], writes=[fsbT[d]])
                    OP("act", lambda e, d=d, pst=pst: e.activation(out=sq2[:, d, :], in_=pst[:], func=AF.Square), reads=[pstT], writes=[sq2T[d]])
                    yield
                ps, psT = banks.get()
                for k in range(NK):
                    OP("pe", lambda e, k=k, ps=ps: e.matmul(ps[:], lhsT=C["ones_bf"], rhs=sq2[:, k, :], start=(k == 0), stop=(k == NK - 1)),
                       reads=[sq2T[k], cT], writes=[psT], signal=(k == NK - 1))
                rstd_from_ss(P, ps[:], psT, rstd2[:], rstd2T, D)
                yield
                for d in range(8):
                    jb = d % 2
                    OP("dve", lambda e, d=d, jb=jb: e.scalar_tensor_tensor(out=tmp[jb][:], in0=fsb[:, d, :], scalar=gpost[:, d:d + 1], in1=rstd2[:], op0=ALU.mult, op1=ALU.mult),
                       reads=[fsbT[d], rstd2T, prT], writes=[tmpT[jb]])
                    OP("pool", lambda e, d=d, jb=jb: e.tensor_tensor(out=fsb[:, d, :], in0=tmp[jb][:], in1=xin4[:, d, :], op=ALU.add), reads=[tmpT[jb], xin4T], writes=[fsbT[d]])
                    if d % 2 == 1:
                        yield
                oT = T()
                outs.append(oT)
                OP("act", lambda e: e.dma_start(out=dstv[:, :, tl * 512:(tl + 1) * 512], in_=fsb[:]), reads=fsbT, writes=[oT], dma=True)
                yield

            def interleave(*gens):
                gens = [g for g in gens if g is not None]
                while gens:
                    for g in list(gens):
                        try:
                            next(g)
                        except StopIteration:
                            gens.remove(g)

            n_ = len(order)
            b2q = []

            def adv_b2(k):
                for _ in range(k):
                    if not b2q:
                        return
                    try:
                        next(b2q[0])
                    except StopIteration:
                        b2q.pop(0)

            for t_ in range(n_ + 2):
                gens = [g for g in (b15(t_ - 2, order[t_ - 2]) if 0 <= t_ - 2 < n_ else None,
                                    b1b(t_ - 1, order[t_ - 1]) if 0 <= t_ - 1 < n_ else None,
                                    b1a(t_, order[t_]) if t_ < n_ else None) if g is not None]
                while gens:
                    adv_b2(1)
                    for g in list(gens):
                        try:
                            next(g)
                        except StopIteration:
                            gens.remove(g)
                adv_b2(2)
                if 0 <= t_ - 2 < n_ and order[t_ - 2] % 4 == 0:
                    b2q.append(b2(order[t_ - 2] // 4))
            while b2q:
                adv_b2(1)
            barrier(P)

    sweepA()
    sweepB()
    return outs


def make_consts():
    k = np.arange(128)[:, None]; i = np.arange(128)[None, :]
    c = np.stack([np.eye(128), (k <= i), (k >= i), (k > i), (k < i), np.ones((128, 128))], axis=1).astype(np.float32)
    return np.ascontiguousarray(c)

def pp(v):
    return np.ascontiguousarray(v.reshape(8, 128).T)

def prep_mixer(I, l):
    return {
        "win": np.ascontiguousarray(I["w_in"][l]), "wout": np.ascontiguousarray(I["w_out"][l]),
        "convw": np.ascontiguousarray(I["conv_w"][l].T.reshape(16, 128, 5).transpose(1, 0, 2)),
        "convb": np.ascontiguousarray(I["conv_b"][l].reshape(16, 128).T),
        "dtbias": np.ascontiguousarray(np.tile(np.concatenate([I["dt_bias_f"][l], I["dt_bias_b"][l]])[None, :], (128, 1))),
        "alog": np.ascontiguousarray(np.tile(np.concatenate([I["a_log_f"][l], I["a_log_b"][l]])[None, :], (128, 1))),
        "dsk": pp(np.repeat(I["d_skip"][l], 64)), "ssdg": pp(I["ssd_norm_g"][l]),
        "gpre": pp(I["mix_pre_g"][l]), "gpost": pp(I["mix_post_g"][l]),
        "lng": np.ascontiguousarray(np.tile(I["gmlp_ln_g"][l][None, :], (128, 1))),
        "lnb": np.ascontiguousarray(np.tile(I["gmlp_ln_b"][l][None, :], (128, 1))),
        "swT": np.ascontiguousarray(I["spatial_w"][l].transpose(2, 0, 1)),
        "sbrow": np.ascontiguousarray(I["spatial_b"][l].reshape(1, 1024)),
    }
MIX_SHAPES = {"win": [1024, 5152], "wout": [2048, 1024], "convw": [128, 16, 5], "convb": [128, 16], "dtbias": [128, 32], "alog": [128, 32],
              "dsk": [128, 8], "ssdg": [128, 8], "gpre": [128, 8], "gpost": [128, 8], "lng": [128, 1024], "lnb": [128, 1024],
              "swT": [128, 8, 128], "sbrow": [1, 1024]}

def declare_mixer(nc, P, sfx):
    d = {k: nc.dram_tensor(f"m{sfx}_{k}", shp, F32, kind="ExternalInput").ap() for k, shp in MIX_SHAPES.items()}
    win_b = nc.dram_tensor(f"m{sfx}_win_b", [1024, 5152], BF16).ap()
    wout_b = nc.dram_tensor(f"m{sfx}_wout_b", [2048, 1024], BF16).ap()
    wT = T()
    for r in range(0, 1024, 256):
        P.op("pool", lambda e, r=r: e.dma_start(out=win_b[r:r + 256].rearrange("r (a b) -> (r a) b", a=4), in_=d["win"][r:r + 256].rearrange("r (a b) -> (r a) b", a=4)),
             writes=[wT], dma=True)
    for r in range(0, 2048, 1024):
        P.op("pool", lambda e, r=r: e.dma_start(out=wout_b[r:r + 1024], in_=d["wout"][r:r + 1024]), writes=[wT], dma=True)
    prm = dict(d); prm["win_b"] = win_b; prm["wout_b"] = wout_b; prm["wT"] = [wT]
    return prm

def declare_scratch(nc):
    return {"YP": nc.dram_tensor("s_YP", [NCH, 128, 1024], F32).ap(), "ZS": nc.dram_tensor("s_ZS", [NCH, 128, 1024], BF16).ap(),
            "YG": nc.dram_tensor("s_YG", [NCH, 128, 1024], BF16).ap(), "CT": nc.dram_tensor("s_CT", [NCH, 128, 512], BF16).ap(),
            "SM": nc.dram_tensor("s_SM", [NCH, 128, 32], F32).ap(), "STB": nc.dram_tensor("s_STB", [NCH, 128, 1024], F32).ap()}


from concourse.bass_utils import run_bass_kernel_spmd

DEPTH = 4
N_CORES = 8
FFN_KEYS = ("wg", "wu", "wd")


def lay_gu(w):
    return np.ascontiguousarray(w.reshape(8, 128, NM, 128).transpose(2, 1, 0, 3).reshape(NM, 128, 1024))


def declare_ffn(nc, tag):
    d = {k: nc.dram_tensor(f"{tag}_{k}", [NM, 128, 1024], F32, kind="ExternalInput").ap() for k in FFN_KEYS}
    d["gpre"] = nc.dram_tensor(f"{tag}_gpre", [128, 8], F32, kind="ExternalInput").ap()
    d["gpost"] = nc.dram_tensor(f"{tag}_gpost", [128, 8], F32, kind="ExternalInput").ap()
    for k in FFN_KEYS:
        d[k + "_b"] = nc.dram_tensor(f"{tag}_{k}_b", [NM, 128, 1024], BF16).ap()
    d["wT"] = [T() for _ in FFN_KEYS]
    return d


def cast_ffn(P, d):
    pieces = []
    for j, k in enumerate(FFN_KEYS):
        for i in range(0, NM, 6):
            n = min(6, NM - i)
            pieces.append(lambda i=i, k=k, j=j, n=n: P.op("pool", lambda e: e.dma_start(out=d[k + "_b"][i:i + n], in_=d[k][i:i + n]),
                                                       writes=[d["wT"][j]], dma=True))
    return pieces


def declare_mixer2(nc, sfx):
    d = {k: nc.dram_tensor(f"m{sfx}_{k}", shp, F32, kind="ExternalInput").ap() for k, shp in MIX_SHAPES.items()}
    d["win_b"] = nc.dram_tensor(f"m{sfx}_win_b", [1024, 5152], BF16).ap()
    d["wout_b"] = nc.dram_tensor(f"m{sfx}_wout_b", [2048, 1024], BF16).ap()
    d["wT"] = [T()]
    return d


def cast_mixer(P, d):
    wT = d["wT"][0]
    pieces = []
    for r in range(0, 1024, 128):
        pieces.append(lambda r=r: P.op("pool", lambda e: e.dma_start(out=d["win_b"][r:r + 128].rearrange("r (a b) -> (r a) b", a=4),
                                                                    in_=d["win"][r:r + 128].rearrange("r (a b) -> (r a) b", a=4)), writes=[wT], dma=True))
    for r in range(0, 2048, 512):
        pieces.append(lambda r=r: P.op("pool", lambda e: e.dma_start(out=d["wout_b"][r:r + 512], in_=d["wout"][r:r + 512]), writes=[wT], dma=True))
    return pieces


HOOKS = []


def run_hooks(n):
    for _ in range(min(n, len(HOOKS))):
        HOOKS.pop(0)()


def build_program():
    nc = bass.Bass("TRN2", target_bir_lowering=False)
    x = nc.dram_tensor("x", [D, L], F32, kind="ExternalInput").ap()
    consts = nc.dram_tensor("consts", [128, 6, 128], F32, kind="ExternalInput").ap()
    y = nc.dram_tensor("y", [D, L], F32, kind="ExternalOutput").ap()
    X = nc.dram_tensor("Xres", [D, L], F32).ap()
    P = Prog(nc, nsem_compute=8, nsem_dma=12)
    with ExitStack() as es:
        banks = Banks(nc, es)
        C = load_consts(P, nc, es, consts)
        scr = declare_scratch(nc)
        stages = []
        for l in range(DEPTH):
            stages.append(("ffn", declare_ffn(nc, f"l{l}f1")))
            stages.append(("mix", declare_mixer2(nc, f"{l}")))
            stages.append(("ffn", declare_ffn(nc, f"l{l}f2")))

        def cast(i):
            kind, d = stages[i]
            return (cast_ffn if kind == "ffn" else cast_mixer)(P, d)

        for pc in cast(0):
            pc()
        outs = []
        for i, (kind, d) in enumerate(stages):
            HOOKS.clear()
            if i + 1 < len(stages):
                HOOKS.extend(cast(i + 1))
            src = x if i == 0 else X
            dst = y if i == len(stages) - 1 else X
            if kind == "ffn":
                outs = ffn_stage(P, nc, banks, src, dst, d["wg_b"], d["wu_b"], d["wd_b"], d["gpre"], d["gpost"], d["wT"], C["ones_bf"], C["T"])
            else:
                outs = mixer_stage(P, nc, banks, src, dst, d, scr, C)
            run_hooks(len(HOOKS))
        P.build(final_waits=outs)
    return nc


def kernel(**I):
    I = {k: np.asarray(v) for k, v in I.items()}
    shared = {"consts": make_consts()}
    for l in range(DEPTH):
        for f in ("ff1", "ff2"):
            tag = f"l{l}f{f[2]}"
            shared[f"{tag}_wg"] = lay_gu(I[f + "_w_gate"][l])
            shared[f"{tag}_wu"] = lay_gu(I[f + "_w_up"][l])
            shared[f"{tag}_wd"] = np.ascontiguousarray(I[f + "_w_down"][l].reshape(NM, 128, 1024))
            shared[f"{tag}_gpre"] = pp(I[f + "_pre_g"][l])
            shared[f"{tag}_gpost"] = pp(I[f + "_post_g"][l])
        for k, v in prep_mixer(I, l).items():
            shared[f"m{l}_{k}"] = v
    x = I["x"]
    in_maps = []
    for b in range(N_CORES):
        m = dict(shared)
        m["x"] = np.ascontiguousarray(x[b].T)
        in_maps.append(m)
    nc = build_program()
    res = run_bass_kernel_spmd(nc, in_maps, core_ids=list(range(N_CORES)))
    out = np.stack([np.ascontiguousarray(r["y"].T) for r in res.results], axis=0)
    return out.astype(np.float32)
```
